# Optimizing a Trainium2 kernel written in Bass

```python
import jax, jax.numpy as jnp
from jax import lax
import numpy as np

D_MODEL = 1024
BATCH = 2
SEQ = 8192
DEPTH = 1

D_MIX = D_MODEL
D_POOL = D_MIX // 2
D_LRU = D_MIX - D_POOL
POOL_WINDOWS = (2, 4, 8, 16)
N_POOL_GROUPS = len(POOL_WINDOWS)
POOL_GROUP_WIDTH = D_POOL // N_POOL_GROUPS
LRU_HEADS = 8
LRU_BLOCK = D_LRU // LRU_HEADS
CONV_WIDTH = 4
LRU_C = 8.0
D_FF = 4 * D_MODEL
PLE_DIM = 256
RMS_EPS = 1e-6
D_IN_PROJ = D_POOL + D_LRU + D_LRU

kernel_name = "hybrid_pool_rglru_block"


def rms_norm(x, g):
    xf = x.astype(jnp.float32)
    y = xf * lax.rsqrt(jnp.mean(xf * xf, axis=-1, keepdims=True) + RMS_EPS)
    return (y * g.astype(jnp.float32)).astype(x.dtype)


def multiscale_pool_mixer(u, pool_w, pool_b, pool_scale):
    B, S, _ = u.shape
    ug = u.reshape(B, S, N_POOL_GROUPS, POOL_GROUP_WIDTH).astype(jnp.float32)
    csum = jnp.cumsum(ug, axis=1)
    t = jnp.arange(S)
    outs = []
    for g, w in enumerate(POOL_WINDOWS):
        c = csum[:, :, g]
        c_lag = jnp.pad(c, ((0, 0), (w, 0), (0, 0)))[:, :S]
        count = jnp.minimum(t + 1, w).astype(jnp.float32)
        outs.append((c - c_lag) / count[None, :, None] - ug[:, :, g])
    d = jnp.stack(outs, axis=2).astype(u.dtype)
    y = jnp.einsum('bsgc,gcd->bsgd', d, pool_w) + pool_b
    return y.reshape(B, S, D_POOL) * pool_scale


def causal_depthwise_conv(u, w, b):
    S = u.shape[1]
    upad = jnp.pad(u, ((0, 0), (CONV_WIDTH - 1, 0), (0, 0)))
    y = b
    for k in range(CONV_WIDTH):
        y = y + upad[:, k:k + S] * w[k]
    return y


def _linear_recurrence_combine(c1, c2):
    a1, b1 = c1
    a2, b2 = c2
    return a1 * a2, a2 * b1 + b2


def rg_lru(u, gate_a_w, gate_a_b, gate_x_w, gate_x_b, lru_L):
    B, S, W = u.shape
    uh = u.reshape(B, S, LRU_HEADS, LRU_BLOCK)
    r = jax.nn.sigmoid(jnp.einsum('bshi,hij->bshj', uh, gate_a_w) + gate_a_b).reshape(B, S, W)
    i = jax.nn.sigmoid(jnp.einsum('bshi,hij->bshj', uh, gate_x_w) + gate_x_b).reshape(B, S, W)
    log_a = LRU_C * r.astype(jnp.float32) * jax.nn.log_sigmoid(lru_L.astype(jnp.float32))
    a = jnp.exp(log_a)
    mult = jnp.sqrt(-jnp.expm1(2.0 * log_a))
    is_first = (jnp.arange(S) == 0)[None, :, None]
    mult = jnp.where(is_first, 1.0, mult)
    b = mult * (i * u).astype(jnp.float32)
    _, h = lax.associative_scan(_linear_recurrence_combine, (a, b), axis=1)
    return h.astype(u.dtype)


def setup_inputs(seed: int = 0) -> dict:
    key = jax.random.key(seed)
    ks = jax.random.split(key, 24)
    f32 = jnp.float32
    nrm = lambda k, shape, scale: jax.random.normal(k, shape, f32) * scale
    gain = lambda k, shape: 1.0 + 0.05 * jax.random.normal(k, shape, f32)
    L = DEPTH
    rad = jnp.sqrt(jax.random.uniform(ks[12], (L, D_LRU), f32, 0.9 ** 2, 0.999 ** 2))
    lru_L = jnp.log(rad) - jnp.log1p(-rad)
    return {
        "x": jax.random.normal(ks[0], (BATCH, SEQ, D_MODEL), f32),
        "p": jax.random.normal(ks[1], (DEPTH, BATCH, SEQ, PLE_DIM), f32),
        "norm_mix_g": gain(ks[2], (L, D_MODEL)),
        "w_in": nrm(ks[3], (L, D_MODEL, D_IN_PROJ), D_MODEL ** -0.5),
        "pool_w": nrm(ks[4], (L, N_POOL_GROUPS, POOL_GROUP_WIDTH, POOL_GROUP_WIDTH), POOL_GROUP_WIDTH ** -0.5),
        "pool_b": nrm(ks[5], (L, N_POOL_GROUPS, POOL_GROUP_WIDTH), 0.01),
        "pool_scale": gain(ks[6], (L, D_POOL)),
        "conv_w": nrm(ks[7], (L, CONV_WIDTH, D_LRU), CONV_WIDTH ** -0.5),
        "conv_b": nrm(ks[8], (L, D_LRU), 0.01),
        "gate_a_w": nrm(ks[9], (L, LRU_HEADS, LRU_BLOCK, LRU_BLOCK), LRU_BLOCK ** -0.5),
        "gate_a_b": nrm(ks[10], (L, LRU_HEADS, LRU_BLOCK), 0.01),
        "gate_x_w": nrm(ks[11], (L, LRU_HEADS, LRU_BLOCK, LRU_BLOCK), LRU_BLOCK ** -0.5),
        "gate_x_b": nrm(ks[13], (L, LRU_HEADS, LRU_BLOCK), 0.01),
        "lru_L": lru_L,
        "w_out": nrm(ks[14], (L, D_MIX, D_MODEL), D_MIX ** -0.5),
        "norm_mlp_g": gain(ks[15], (L, D_MODEL)),
        "w_up": nrm(ks[16], (L, D_MODEL, D_FF), D_MODEL ** -0.5),
        "w_down": nrm(ks[17], (L, D_FF, D_MODEL), D_FF ** -0.5),
        "norm_ple_g": gain(ks[18], (L, D_MODEL)),
        "w_ple_gate": nrm(ks[19], (L, D_MODEL, D_MODEL), D_MODEL ** -0.5),
        "b_ple_gate": nrm(ks[20], (L, D_MODEL), 0.01),
        "w_ple_proj": nrm(ks[21], (L, PLE_DIM, D_MODEL), PLE_DIM ** -0.5),
        "norm_final_g": gain(ks[22], (D_MODEL,)),
    }


def reference(x, p, norm_mix_g, w_in, pool_w, pool_b, pool_scale, conv_w, conv_b,
              gate_a_w, gate_a_b, gate_x_w, gate_x_b, lru_L, w_out, norm_mlp_g,
              w_up, w_down, norm_ple_g, w_ple_gate, b_ple_gate, w_ple_proj, norm_final_g):
    h = x
    for l in range(DEPTH):
        z = rms_norm(h, norm_mix_g[l])
        proj = z @ w_in[l]
        u_pool = proj[..., :D_POOL]
        u_lru = proj[..., D_POOL:D_POOL + D_LRU]
        u_gate = proj[..., D_POOL + D_LRU:]
        y_pool = multiscale_pool_mixer(u_pool, pool_w[l], pool_b[l], pool_scale[l])
        xb = causal_depthwise_conv(u_lru, conv_w[l], conv_b[l])
        y_lru = rg_lru(xb, gate_a_w[l], gate_a_b[l], gate_x_w[l], gate_x_b[l], lru_L[l])
        y_lru = y_lru * jax.nn.gelu(u_gate)
        h = h + jnp.concatenate([y_pool, y_lru], axis=-1) @ w_out[l]
        z = rms_norm(h, norm_mlp_g[l])
        h = h + jnp.square(jax.nn.relu(z @ w_up[l])) @ w_down[l]
        z = rms_norm(h, norm_ple_g[l])
        gate = jax.nn.sigmoid(z @ w_ple_gate[l] + b_ple_gate[l])
        h = h + gate * (p[l] @ w_ple_proj[l])
    return rms_norm(h, norm_final_g)
```

```python
import numpy as np
import concourse.bass as bass
import concourse.mybir as mybir
from concourse.bass_utils import run_bass_kernel_spmd

F32 = mybir.dt.float32
BF16 = mybir.dt.bfloat16
AF = mybir.ActivationFunctionType
ALU = mybir.AluOpType
AX = mybir.AxisListType
DSZ = {F32: 4, BF16: 2}

NCORES = 8
T = 2048
D = 1024
NT = 16
NB = 4
KC = 8
DFF = 4096
PLE = 256
EPS = 1e-6
HALO = 16
SQ_ENG = "act"
import os
NOCC = bool(os.environ.get("NOCC"))
WMODE = os.environ.get("WMODE", "sw")
CCG = int(os.environ.get("CCG", "4"))


class Op:
    __slots__ = ("eng", "fn", "deps", "sem", "val", "signal", "is_dma", "inc", "name", "idx", "key")

    def __init__(self, eng, fn, is_dma=False, name=""):
        self.eng = eng
        self.fn = fn
        self.deps = []
        self.sem = None
        self.val = None
        self.signal = False
        self.is_dma = is_dma
        self.inc = 1
        self.name = name
        self.idx = -1
        self.key = None


def ap_intervals(ap, is_dram=False, cap=64):
    dsz = DSZ[ap.dtype]
    dims = list(ap.ap)
    off = int(ap.offset)
    if not is_dram:
        pstep = dims[0][0]
        dims = dims[1:]
        if pstep > 0:
            off = off % pstep
    dims = [(s, c) for (s, c) in dims if c > 1]
    run = 1
    while dims and dims[-1][0] == run:
        run *= dims[-1][1]
        dims.pop()
    n_outer = 1
    for s, c in dims:
        n_outer *= c
    if n_outer > cap or any(s < 0 for s, c in dims):
        lo = off + sum(min(0, s * (c - 1)) for s, c in dims)
        hi = off + sum(max(0, s * (c - 1)) for s, c in dims) + run
        return [(lo * dsz, hi * dsz)]
    offs = [off]
    for s, c in dims:
        offs = [o + s * i for o in offs for i in range(c)]
    return [(o * dsz, (o + run) * dsz) for o in offs]


class Prog:
    ENGS = ["pe", "act", "dve", "pool", "sp"]

    def __init__(self, nc, same_engine_sync=True):
        self.nc = nc
        self.ops = {e: [] for e in self.ENGS}
        self.hist = {}
        self.dma_keys = {}
        self.eng_sem = {}
        self.same_engine_sync = same_engine_sync
        self.out_dma_ops = []
        self._sem_ctx = []

    def _new_sem(self, name):
        cm = self.nc.semaphore(name)
        s = cm.__enter__()
        self._sem_ctx.append(cm)
        return s

    def _regions(self, aps):
        out = []
        for a in aps:
            if a is None:
                continue
            space = str(a.space)
            is_dram = "dram" in space.lower()
            out.append((a.tensor.name, ap_intervals(a, is_dram=is_dram)))
        return out

    def _add_deps(self, op, reads, writes):
        deps = []
        rregs = self._regions(reads)
        wregs = self._regions(writes)
        for key, ivs in rregs:
            h = self.hist.setdefault(key, [])
            for lo, hi in ivs:
                for e in h:
                    if e[2] and e[0] < hi and lo < e[1]:
                        deps.append(e[3])
        for key, ivs in wregs:
            h = self.hist.setdefault(key, [])
            for lo, hi in ivs:
                for e in h:
                    if e[0] < hi and lo < e[1]:
                        deps.append(e[3])
        for key, ivs in rregs:
            h = self.hist[key]
            for lo, hi in ivs:
                h.append([lo, hi, False, op])
        for key, ivs in wregs:
            h = self.hist[key]
            for lo, hi in ivs:
                h[:] = [e for e in h if not (lo <= e[0] and e[1] <= hi)]
                h.append([lo, hi, True, op])
        best = {}
        for d in deps:
            if d is op:
                continue
            if (not d.is_dma) and d.eng == op.eng and (not op.is_dma):
                if op.eng == "pe" or not self.same_engine_sync:
                    continue
            k = ("dma", d.key) if d.is_dma else ("eng", d.eng)
            if k not in best or d.idx > best[k].idx:
                best[k] = d
        have = {id(d) for d in op.deps}
        for d in best.values():
            if id(d) not in have:
                op.deps.append(d)

    def op(self, eng, fn, reads=(), writes=(), name=""):
        o = Op(eng, fn, name=name)
        o.idx = len(self.ops[eng])
        self._add_deps(o, reads, writes)
        self.ops[eng].append(o)
        return o

    def wait_for(self, eng, ops):
        o = Op(eng, None, name="waitfor")
        o.idx = len(self.ops[eng])
        o.deps = [d for d in ops if d is not None]
        self.ops[eng].append(o)
        return o

    def dma(self, queue, out, in_, key, is_output=False, fn=None, inc=16):
        if fn is None:
            fn = lambda e, out=out, in_=in_: e.dma_start(out=out, in_=in_)
        o = Op(queue, fn, is_dma=True, name="dma:" + key)
        if key not in self.dma_keys:
            self.dma_keys[key] = [self._new_sem("d_" + key), 0, None]
        k = self.dma_keys[key]
        o.key = key
        o.idx = k[1]
        if k[2] is not None:
            o.deps.append(k[2])
        self._add_deps(o, [in_], [out])
        k[1] += inc
        k[2] = o
        o.sem = k[0]
        o.val = k[1]
        o.inc = inc
        o.signal = True
        self.ops[queue].append(o)
        if is_output:
            self.out_dma_ops.append(o)
        return o

    def emit(self):
        nc = self.nc
        for e in self.ENGS:
            self.eng_sem[e] = self._new_sem("e_" + e)
        fin = Op("sp", None, name="final")
        fin.deps = list(self.out_dma_ops) + [k[2] for k in self.dma_keys.values() if k[2] is not None]
        self.ops["sp"].append(fin)
        for e in self.ENGS:
            for o in self.ops[e]:
                for d in o.deps:
                    if not d.is_dma:
                        d.signal = True
        for e in self.ENGS:
            cnt = 0
            for o in self.ops[e]:
                if o.is_dma:
                    continue
                if o.signal:
                    cnt += 1
                    o.sem = self.eng_sem[e]
                    o.val = cnt
        stats = {}

        def run(ename, eng):
            waited = {}
            nw = 0
            for o in self.ops[ename]:
                need = {}
                for d in o.deps:
                    s = d.sem
                    if d.val > need.get(id(s), (None, 0))[1]:
                        need[id(s)] = (s, d.val)
                for sid, (s, v) in need.items():
                    if waited.get(sid, 0) >= v:
                        continue
                    eng.wait_ge(s, v)
                    waited[sid] = v
                    nw += 1
                if o.fn is None:
                    continue
                ins = o.fn(eng)
                if o.signal:
                    ins.then_inc(o.sem, o.inc)
            stats[ename] = (len(self.ops[ename]), nw)

        with nc.Block() as block:

            @block.tensor
            def _(e):
                run("pe", e)

            @block.scalar
            def _(e):
                run("act", e)

            @block.vector
            def _(e):
                run("dve", e)

            @block.gpsimd
            def _(e):
                run("pool", e)

            @block.sync
            def _(e):
                run("sp", e)

        for cm in reversed(self._sem_ctx):
            cm.__exit__(None, None, None)
        self._sem_ctx = []
        self.stats = stats
        return stats


class Arena:
    def __init__(self, nc, nbytes):
        self.nc = nc
        self.nbytes = nbytes
        self.t = nc.alloc_sbuf_tensor("arena", [128, nbytes // 4], F32)
        self.cur = 0
        self.peak = 0

    def alloc(self, free_shape, dtype, at=None):
        ne = int(np.prod(free_shape))
        n = ne * DSZ[dtype]
        n = (n + 63) // 64 * 64
        if at is None:
            at = self.cur
            self.cur += n
        assert at % 4 == 0
        assert at + n <= self.nbytes, f"arena overflow {at + n} > {self.nbytes}"
        self.peak = max(self.peak, at + n)
        v = self.t[:, at // 4:(at + n) // 4]
        if dtype != F32:
            v = v.bitcast(dtype)
        v = v[:, 0:ne]
        if len(free_shape) == 2:
            v = v.rearrange("p (a b) -> p a b", a=free_shape[0])
        elif len(free_shape) == 3:
            v = v.rearrange("p (a b c) -> p a b c", a=free_shape[0], b=free_shape[1])
        return v


def build_nc(stage="full"):
    nc = bass.Bass("TRN2", target_bir_lowering=False)

    def din(name, shape):
        return nc.dram_tensor(name, shape, F32, kind="ExternalInput").ap()

    x_d = din("x", [T, D])
    xh_d = din("xh", [HALO, D])
    p_d = din("p", [T, PLE])
    w_in_d = din("w_in", [D, 1536])
    w_out_d = din("w_out", [D, D])
    w_up_d = din("w_up", [D, DFF])
    w_dn_d = din("w_down", [DFF, D])
    w_pg_d = din("w_pg", [D, D])
    w_pp_d = din("w_pp", [PLE, D])
    pw_d = din("pool_w", [4, 128, 128])
    gaw_d = din("gaw", [4, 128, 128])
    gxw_d = din("gxw", [4, 128, 128])
    vec_d = din("vecs", [128, 40])
    gbc_d = din("gbc", [128, 5 * D])
    id_d = din("ident", [128, 128])
    cm_d = din("cmask", [128, 66])
    ic_d = din("invc", [128, 64])
    out_d = nc.dram_tensor("out", [T, D], F32, kind="ExternalOutput").ap()
    cc_in = nc.dram_tensor("cc_in", [128, 8], F32)
    cc_out = nc.dram_tensor("cc_out", [CCG * 128, 8], F32)

    P = Prog(nc)
    A = Arena(nc, 212736)

    HREG = A.alloc([NT * D], F32)
    H = HREG.rearrange("p (t d) -> p t d", t=NT)
    ZT = A.alloc([KC, T], BF16)
    SLOT = [A.alloc([8192], BF16) for _ in range(4)]
    XR = [A.alloc([D], F32) for _ in range(2)]
    GS = A.alloc([D], F32)
    VEC = A.alloc([40], F32)
    DV = A.alloc([24], F32)
    CM = A.alloc([66], F32)
    INVC = A.alloc([4, 16], F32)
    IDB = A.alloc([128], BF16)
    PW = A.alloc([4, 128], BF16)
    GAW = A.alloc([4, 128], BF16)
    GXW = A.alloc([4, 128], BF16)
    SS = A.alloc([32], F32)
    RMS = A.alloc([32], F32)
    RSTD = A.alloc([32], F32)
    RS = A.alloc([16], F32)
    RSS = A.alloc([4], F32)
    SUMM = A.alloc([8], F32)
    GATH = A.alloc([CCG, 8], F32)
    APR = A.alloc([CCG, 4], F32)
    HPR = A.alloc([CCG, 4], F32)
    FOLD = A.alloc([4, CCG], F32)
    CAR = A.alloc([4], F32)
    TAIL = A.alloc([4, 16], F32)
    ZTH = A.alloc([KC, HALO], BF16)
    E12 = A.alloc([8], F32)
    EPSC = A.alloc([1], F32)
    SCR = A.cur
    IDF = A.alloc([128], F32, at=SCR + 8192)
    A.peak_persist = SCR

    ps = [nc.alloc_psum_tensor(f"ps{i}", [128, 512], F32) for i in range(8)]
    TRB = [ps[i][:, :].bitcast(BF16) for i in range(2)]
    rr = {"A": 0, "B": 0, "T": 0}

    def bankA():
        b = ps[2 + rr["A"] % 3][:, :]
        rr["A"] += 1
        return b

    def bankB():
        b = ps[5 + rr["B"] % 3][:, :]
        rr["B"] += 1
        return b

    def bankT():
        b = TRB[rr["T"] % 2]
        rr["T"] += 1
        return b

    def col(ap, i):
        return ap[:, i:i + 1]

    P.dma("sp", VEC, vec_d, "c_vec")
    P.dma("sp", CM, cm_d, "c_cm")
    P.dma("sp", INVC, ic_d.rearrange("p (g t) -> p g t", g=4), "c_ic")
    P.dma("sp", IDF, id_d, "c_id")
    P.dma("sp", GS, gbc_d[:, 0:D], "gs")
    P.op("pool", lambda e: e.tensor_copy(out=IDB, in_=IDF), [IDF], [IDB])
    WIL = SLOT[0][:, 0:4096].rearrange("p (k n) -> p k n", k=KC)
    WIP = SLOT[1][:, 0:4096].rearrange("p (k n) -> p k n", k=KC)
    WIG = SLOT[1][:, 4096:8192].rearrange("p (k n) -> p k n", k=KC)
    P.dma("pool", WIL, w_in_d[:, 512:1024].rearrange("(k p) n -> p k n", p=128), "w_s0")
    P.dma("pool", GAW, gaw_d.rearrange("c k m -> k c m"), "c_gaw")
    P.dma("pool", GXW, gxw_d.rearrange("c k m -> k c m"), "c_gxw")

    LCOL = VEC[:, 36:40]
    P.op("act", lambda e: e.activation(out=E12[:, 0:4], in_=LCOL, func=AF.Exp, scale=-1.0), [LCOL], [E12[:, 0:4]])
    P.op("act", lambda e: e.activation(out=E12[:, 4:8], in_=E12[:, 0:4], func=AF.Ln, bias=1.0), [E12[:, 0:4]], [E12[:, 4:8]])
    SP_ = E12[:, 4:8]
    P.op("dve", lambda e: e.tensor_scalar(out=DV[:, 8:12], in0=SP_, scalar1=-8.0, scalar2=None, op0=ALU.mult), [SP_], [DV[:, 8:12]])
    P.op("dve", lambda e: e.tensor_scalar(out=DV[:, 12:16], in0=SP_, scalar1=-4.0, scalar2=None, op0=ALU.mult), [SP_], [DV[:, 12:16]])
    P.op("dve", lambda e: e.tensor_scalar(out=DV[:, 16:20], in0=SP_, scalar1=-8192.0, scalar2=None, op0=ALU.mult), [SP_], [DV[:, 16:20]])
    P.op("dve", lambda e: e.tensor_scalar(out=DV[:, 0:4], in0=VEC[:, 28:32], scalar1=0.5, scalar2=None, op0=ALU.mult), [VEC[:, 28:32]], [DV[:, 0:4]])
    P.op("dve", lambda e: e.tensor_scalar(out=DV[:, 4:8], in0=VEC[:, 32:36], scalar1=0.5, scalar2=None, op0=ALU.mult), [VEC[:, 32:36]], [DV[:, 4:8]])
    HBA = lambda c: col(DV, 0 + c)
    HBX = lambda c: col(DV, 4 + c)
    C1 = lambda c: col(DV, 8 + c)
    HC1 = lambda c: col(DV, 12 + c)
    KC1 = lambda c: col(DV, 16 + c)
    PB_ = lambda g: col(VEC, 0 + g)
    PSC = lambda g: col(VEC, 4 + g)
    CW = lambda c, k: col(VEC, 8 + c * 4 + k)
    CB = lambda c: col(VEC, 24 + c)

    def transposes_to(zb, dst3, npart=128, ncols=128, nchunk=KC):
        tb_ = bankT()
        for kc in range(nchunk):
            o = tb_[:, kc * ncols:(kc + 1) * ncols]
            i = zb[0:npart, kc * 128:(kc + 1) * 128]
            idn = IDB[0:npart, 0:npart]
            P.op("pe", lambda e, o=o, i=i, idn=idn: e.transpose(o, i, idn), [i, idn], [o])
        src = tb_[:, 0:nchunk * ncols].rearrange("p (k n) -> p k n", k=nchunk)
        P.op("act", lambda e, src=src: e.copy(out=dst3, in_=src), [src], [dst3])

    def sumsq(src, sscol, SQ, npart=128):
        s_ = src[0:npart]
        q_ = SQ[0:npart]
        c_ = sscol[0:npart]
        P.op("act", lambda e: e.activation(out=q_, in_=s_, func=AF.Square, accum_out=c_), [s_], [q_, c_])

    def rstd_from(sscols, rmscols, rstdcols, npart=128):
        a, b, c = sscols[0:npart], rmscols[0:npart], rstdcols[0:npart]
        P.op("act", lambda e: e.activation(out=b, in_=a, func=AF.Sqrt, scale=1.0 / D, bias=EPSC[0:npart]), [a, EPSC], [b])
        P.op("dve", lambda e: e.reciprocal(out=c, in_=b), [b], [c])

    def scale_to(dst, src, rstdcol, gain, npart=128):
        d_, s_, r_, g_ = dst[0:npart], src[0:npart], rstdcol[0:npart], gain[0:npart]
        P.op("dve", lambda e: e.scalar_tensor_tensor(out=d_, in0=s_, scalar=r_, in1=g_, op0=ALU.mult, op1=ALU.mult),
             [s_, r_, g_], [d_])

    P.op("pool", lambda e: e.memset(EPSC, EPS), [], [EPSC])

    WUP = [None] * 4
    WDN = [None] * 4
    up_slot = {0: 3, 1: 1, 2: 3, 3: 1}
    dn_slot = {0: 0, 1: 2, 2: 0, 3: 2}

    xri = [0]

    def xr_next():
        k = xri[0] % 2
        xri[0] += 1
        return k

    WQ = []

    def enqueue(dst3, src2d, nk):
        for k in range(nk):
            WQ.append((dst3[:, k, :], src2d[k * 128:(k + 1) * 128, :]))

    def pump(n):
        for _ in range(min(n, len(WQ))):
            dst, src = WQ.pop(0)
            sl = xr_next()
            P.dma("sp", XR[sl], src, f"xr{sl}")
            P.op("pool", lambda e, dst=dst, sl=sl: e.tensor_copy(out=dst, in_=XR[sl]), [XR[sl]], [dst])

    def load_wup(g):
        v = SLOT[up_slot[g]].rearrange("p (k n) -> p k n", k=KC)
        if WMODE == "sw":
            P.dma("pool", v, w_up_d[:, g * 1024:(g + 1) * 1024].rearrange("(k p) n -> p k n", p=128), f"w_s{up_slot[g]}")
        else:
            enqueue(v, w_up_d[:, g * 1024:(g + 1) * 1024], KC)
        WUP[g] = v

    def load_wdn(g):
        v = SLOT[dn_slot[g]].rearrange("p (k n) -> p k n", k=KC)
        if WMODE == "sw":
            P.dma("pool", v, w_dn_d[g * 1024:(g + 1) * 1024, :].rearrange("(k p) n -> p k n", p=128), f"w_s{dn_slot[g]}")
        else:
            enqueue(v, w_dn_d[g * 1024:(g + 1) * 1024, :], KC)
        WDN[g] = v

    MONLY = stage.startswith("M")
    if not MONLY:
        A.cur = SCR
        ZB = [A.alloc([D], BF16) for _ in range(2)]
        SQ = A.alloc([D], BF16)
        sl = xr_next()
        XH = XR[sl]
        P.dma("sp", XH[0:HALO], xh_d, f"xr{sl}")
        sumsq(XH, col(SS, 16), SQ, HALO)
        rstd_from(col(SS, 16), col(RMS, 16), col(RSTD, 16), HALO)
        scale_to(ZB[1], XH, col(RSTD, 16), GS, HALO)
        transposes_to(ZB[1], ZTH, npart=HALO, ncols=HALO)
        XA = [A.alloc([D], F32) for _ in range(5)] + [XR[0], XR[1]]
        NXA = len(XA)

        def a_load(tt):
            P.dma("sp", XA[tt % NXA], x_d[tt * 128:(tt + 1) * 128, :], f"xa{tt % NXA}")

        def a_stats_act(tt):
            sumsq(XA[tt % NXA], col(SS, tt), SQ)
            a_, b_ = col(SS, tt), col(RMS, tt)
            P.op("act", lambda e, a_=a_, b_=b_: e.activation(out=b_, in_=a_, func=AF.Sqrt, scale=1.0 / D, bias=EPSC), [a_, EPSC], [b_])

        def a_recip(tt):
            b_, c_ = col(RMS, tt), col(RSTD, tt)
            P.op("dve", lambda e, b_=b_, c_=c_: e.reciprocal(out=c_, in_=b_), [b_], [c_])

        def a_stage2(tt):
            s = tt % 2
            scale_to(ZB[s], XA[tt % NXA], col(RSTD, tt), GS)
            transposes_to(ZB[s], ZT[:, :, tt * 128:(tt + 1) * 128])

        for tt in range(NXA - 1):
            a_load(tt)
        for tt in range(2):
            a_stats_act(tt)
            a_recip(tt)
        for tt in range(NT):
            if tt + 2 < NT:
                a_stats_act(tt + 2)
            a_stage2(tt)
            if tt + 2 < NT:
                a_recip(tt + 2)
            if tt + NXA - 1 < NT:
                a_load(tt + NXA - 1)

        P.dma("pool", WIP, w_in_d[:, 0:512].rearrange("(k p) n -> p k n", p=128), "w_s1a")
        P.dma("pool", WIG, w_in_d[:, 1024:1536].rearrange("(k p) n -> p k n", p=128), "w_s1b")
        P.dma("pool", PW, pw_d.rearrange("g c d -> c g d"), "c_pw")
        A.cur = SCR
        XB = [A.alloc([512], F32) for _ in range(2)]
        XBB = [A.alloc([512], BF16) for _ in range(2)]
        THR = [A.alloc([512], F32) for _ in range(2)]
        THI = [A.alloc([512], F32) for _ in range(2)]
        SC1 = [A.alloc([512], F32) for _ in range(2)]
        UCs = [SLOT[2].bitcast(F32)[:, 0:HALO + T], SLOT[3].bitcast(F32)[:, 0:HALO + T]]
        HF = HREG

        def Ablk(c, tb):
            return HF[:, tb * 4096 + c * 512: tb * 4096 + (c + 1) * 512]

        def Bblk(c, tb):
            return HF[:, tb * 4096 + 2048 + c * 512: tb * 4096 + 2048 + (c + 1) * 512]

        def b_s1(k):
            c, tb = k // 4, k % 4
            UC = UCs[c % 2]
            if tb == 0:
                pz = bankA()
                for kc in range(KC):
                    l_ = WIL[:, kc, c * 128:(c + 1) * 128]
                    r_ = ZTH[:, kc, :]
                    P.op("pe", lambda e, l_=l_, r_=r_, pz=pz, kc=kc: e.matmul(pz[:, 0:HALO], l_, r_, start=(kc == 0), stop=(kc == KC - 1)),
                         [l_, r_], [pz[:, 0:HALO]])
                P.op("act", lambda e, pz=pz, UC=UC: e.copy(out=UC[:, 0:HALO], in_=pz[:, 0:HALO]), [pz[:, 0:HALO]], [UC[:, 0:HALO]])
            base = HALO + tb * 512
            pz = bankA()
            for kc in range(KC):
                l_ = WIL[:, kc, c * 128:(c + 1) * 128]
                r_ = ZT[:, kc, tb * 512:(tb + 1) * 512]
                P.op("pe", lambda e, l_=l_, r_=r_, pz=pz, kc=kc: e.matmul(pz, l_, r_, start=(kc == 0), stop=(kc == KC - 1)), [l_, r_], [pz])
            ub = UC[:, base:base + 512]
            P.op("act", lambda e, pz=pz, ub=ub: e.copy(out=ub, in_=pz), [pz], [ub])

        def b_s2a(k):
            c, tb = k // 4, k % 4
            UC = UCs[c % 2]
            q = k % 2
            base = HALO + tb * 512
            xb = XB[q]
            u0 = UC[:, base - 3:base - 3 + 512]
            P.op("dve", lambda e, xb=xb, u0=u0, c=c: e.tensor_scalar(out=xb, in0=u0, scalar1=CW(c, 0), scalar2=CB(c), op0=ALU.mult, op1=ALU.add),
                 [u0, VEC], [xb])
            for kk in range(1, 4):
                uk = UC[:, base - 3 + kk:base - 3 + kk + 512]
                P.op("dve", lambda e, xb=xb, uk=uk, c=c, kk=kk: e.scalar_tensor_tensor(out=xb, in0=uk, scalar=CW(c, kk), in1=xb, op0=ALU.mult, op1=ALU.add),
                     [uk, xb, VEC], [xb])
            xbb = XBB[q]
            P.op("act", lambda e, xbb=xbb, xb=xb: e.copy(out=xbb, in_=xb), [xb], [xbb])
            pr = bankA()
            ga = GAW[:, c, :]
            P.op("pe", lambda e, pr=pr, ga=ga, xbb=xbb: e.matmul(pr, ga, xbb, start=True, stop=True), [ga, xbb], [pr])
            thr = THR[q]
            rsc = col(RS, c * 4 + tb)
            P.op("act", lambda e, thr=thr, pr=pr, c=c, rsc=rsc: e.activation(out=thr, in_=pr, func=AF.Tanh, scale=0.5, bias=HBA(c), accum_out=rsc),
                 [pr, DV], [thr, rsc])
            pi = bankA()
            gx = GXW[:, c, :]
            P.op("pe", lambda e, pi=pi, gx=gx, xbb=xbb: e.matmul(pi, gx, xbb, start=True, stop=True), [gx, xbb], [pi])
            thi = THI[q]
            P.op("act", lambda e, thi=thi, pi=pi, c=c: e.activation(out=thi, in_=pi, func=AF.Tanh, scale=0.5, bias=HBX(c)), [pi, DV], [thi])
            ab, bb = Ablk(c, tb), Bblk(c, tb)
            P.op("act", lambda e, ab=ab, thr=thr, c=c: e.activation(out=ab, in_=thr, func=AF.Exp, scale=HC1(c), bias=HC1(c)), [thr, DV], [ab])
            P.op("act", lambda e, bb=bb, thr=thr, c=c: e.activation(out=bb, in_=thr, func=AF.Exp, scale=C1(c), bias=C1(c)), [thr, DV], [bb])

        def b_s2b(k):
            c, tb = k // 4, k % 4
            UC = UCs[c % 2]
            q = k % 2
            thi, xb = THI[q], XB[q]
            qb = UC[:, tb * 512:(tb + 1) * 512]
            P.op("pool", lambda e, thi=thi: e.tensor_scalar(out=thi, in0=thi, scalar1=1.0, scalar2=0.5, op0=ALU.add, op1=ALU.mult), [thi], [thi])
            P.op("pool", lambda e, qb=qb, thi=thi, xb=xb: e.tensor_tensor(out=qb, in0=thi, in1=xb, op=ALU.mult), [thi, xb], [qb])

        prevs = {}

        def b_tail_act(c):
            for tb in range(NB):
                bb = Bblk(c, tb)
                P.op("act", lambda e, bb=bb: e.activation(out=bb, in_=bb, func=AF.Sqrt, scale=-1.0, bias=1.0), [bb], [bb])
            b0 = Bblk(c, 0)[:, 0:1]
            P.op("dve", lambda e, b0=b0: e.tensor_scalar(out=b0, in0=b0, scalar1=col(CM, 1), scalar2=col(CM, 0), op0=ALU.mult, op1=ALU.add),
                 [b0, CM], [b0])

        def b_tail_step(c, tb):
            UC = UCs[c % 2]
            prev = prevs.get(c)
            bb = Bblk(c, tb)
            ab = Ablk(c, tb)
            qb = UC[:, tb * 512:(tb + 1) * 512]
            P.op("pool", lambda e, bb=bb, qb=qb: e.tensor_tensor(out=bb, in0=qb, in1=bb, op=ALU.mult), [qb, bb], [bb])
            so = SC1[tb % 2]
            init = 0.0 if prev is None else prev[:, 511:512]
            rd = [ab, bb] + ([] if prev is None else [prev[:, 511:512]])
            P.op("dve", lambda e, so=so, ab=ab, bb=bb, init=init: e.tensor_tensor_scan(out=so, data0=ab, data1=bb, initial=init, op0=ALU.mult, op1=ALU.add),
                 rd, [so])
            prevs[c] = so
            if tb == NB - 1:
                hl = col(SUMM, 4 + c)
                P.op("dve", lambda e, hl=hl, so=so: e.tensor_copy(out=hl, in_=so[:, 511:512]), [so[:, 511:512]], [hl])
                rs4 = RS[:, c * 4:(c + 1) * 4]
                rss = col(RSS, c)
                P.op("dve", lambda e, rs4=rs4, rss=rss: e.reduce_sum(out=rss, in_=rs4, axis=AX.X), [rs4], [rss])
                at = col(SUMM, c)
                P.op("act", lambda e, at=at, rss=rss, c=c: e.activation(out=at, in_=rss, func=AF.Exp, scale=HC1(c), bias=KC1(c)), [rss, DV], [at])

        b_s1(0)
        b_s1(1)
        b_s2a(0)
        for k in range(16):
            if k + 2 < 16:
                b_s1(k + 2)
            if k + 1 < 16:
                b_s2a(k + 1)
            b_s2b(k)
            if k >= 4:
                b_tail_step(k // 4 - 1, k % 4)
            if k % 4 == 3:
                b_tail_act(k // 4)
        for tb in range(NB):
            b_tail_step(3, tb)

        P.dma("sp", cc_in.ap(), SUMM, "ccin")
        cci = cc_in.ap()
        cco = cc_out.ap()
        if NOCC:
            P.dma("sp", cco[0:128, :], cci, "cc")
        else:
            P.wait_for("pool", [k[2] for kk, k in P.dma_keys.items() if k[2] is not None and k[2].eng == "pool"])
            ccop = P.dma("pool", cco, cci, "cc", inc=1,
                         fn=lambda e: e.collective_compute("AllGather", ALU.bypass, replica_groups=[list(range(g0, g0 + CCG)) for g0 in range(0, NCORES, CCG)],
                                                           ins=[cc_in.ap().opt()], outs=[cc_out.ap().opt()]))
        P.dma("sp", GATH, cco.rearrange("(r p) c -> p r c", p=128), "gath")
        def emit_fold():
            M8 = CM[:, 2:2 + 4 * CCG].rearrange("p (k c) -> p k c", k=CCG)
            NM8 = CM[:, 34:34 + 4 * CCG].rearrange("p (k c) -> p k c", k=CCG)
            GA_ = GATH[:, :, 0:4]
            GH_ = GATH[:, :, 4:8]
            P.op("dve", lambda e: e.tensor_tensor(out=APR, in0=GA_, in1=M8, op=ALU.mult), [GA_, M8], [APR])
            P.op("dve", lambda e: e.tensor_tensor(out=APR, in0=APR, in1=NM8, op=ALU.add), [APR, NM8], [APR])
            P.op("dve", lambda e: e.tensor_tensor(out=HPR, in0=GH_, in1=M8, op=ALU.mult), [GH_, M8], [HPR])
            for c in range(4):
                d0 = APR[:, :, c]
                d1 = HPR[:, :, c]
                fo = FOLD[:, c, :]
                P.op("dve", lambda e, d0=d0, d1=d1, fo=fo: e.tensor_tensor_scan(out=fo, data0=d0, data1=d1, initial=0.0, op0=ALU.mult, op1=ALU.add),
                     [d0, d1], [fo])

        HIN = lambda c: FOLD[:, c, CCG - 1:CCG]

        A.cur = SCR
        YC = A.alloc([8, 512], BF16)
        UPB = [A.alloc([HALO + 512], F32) for _ in range(2)]
        SB_ = [A.alloc([HALO + 512], F32) for _ in range(2)]
        SBP = [A.alloc([HALO + 512], F32) for _ in range(2)]
        DBFS = [A.alloc([512], BF16) for _ in range(2)]
        T16 = A.alloc([16], F32)
        HS = A.alloc([512], F32)
        GGS = [A.alloc([512], F32) for _ in range(2)]
        SQC = SBP[0].bitcast(BF16)[:, 0:D]
        WO = SLOT[2].rearrange("p (k n) -> p k n", k=KC)
        if WMODE == "sw":
            P.dma("pool", WO, w_out_d.rearrange("(k p) n -> p k n", p=128), "w_s2")
        else:
            enqueue(WO, w_out_d, KC)
        load_wup(0)
        load_wdn(0)
        pump(8)
        Y = YC
        L = HALO + 512

        def c_s1(item):
            kind, tb, c, bi = item
            g = c
            pump(1) if kind == "lru" else None
            if kind == "lru":
                GG = GGS[bi % 2]
                pg = bankA()
                for kc in range(KC):
                    l_ = WIG[:, kc, c * 128:(c + 1) * 128]
                    r_ = ZT[:, kc, tb * 512:(tb + 1) * 512]
                    P.op("pe", lambda e, l_=l_, r_=r_, pg=pg, kc=kc: e.matmul(pg, l_, r_, start=(kc == 0), stop=(kc == KC - 1)), [l_, r_], [pg])
                P.op("act", lambda e, pg=pg, GG=GG: e.activation(out=GG, in_=pg, func=AF.Gelu_apprx_tanh), [pg], [GG])
                return
            UP = UPB[bi % 2]
            if tb == 0:
                pz = bankA()
                for kc in range(KC):
                    l_ = WIP[:, kc, g * 128:(g + 1) * 128]
                    r_ = ZTH[:, kc, :]
                    P.op("pe", lambda e, l_=l_, r_=r_, pz=pz, kc=kc: e.matmul(pz[:, 0:HALO], l_, r_, start=(kc == 0), stop=(kc == KC - 1)),
                         [l_, r_], [pz[:, 0:HALO]])
                P.op("act", lambda e, pz=pz, UP=UP: e.copy(out=UP[:, 0:HALO], in_=pz[:, 0:HALO]), [pz[:, 0:HALO]], [UP[:, 0:HALO]])
            pz = bankA()
            for kc in range(KC):
                l_ = WIP[:, kc, g * 128:(g + 1) * 128]
                r_ = ZT[:, kc, tb * 512:(tb + 1) * 512]
                P.op("pe", lambda e, l_=l_, r_=r_, pz=pz, kc=kc: e.matmul(pz, l_, r_, start=(kc == 0), stop=(kc == KC - 1)), [l_, r_], [pz])
            ub = UP[:, HALO:HALO + 512]
            P.op("act", lambda e, pz=pz, ub=ub: e.copy(out=ub, in_=pz), [pz], [ub])

        def c_s2(item):
            kind, tb, c, bi = item
            g = c
            if kind == "lru":
                GG = GGS[bi % 2]
                ab, bb = Ablk(c, tb), Bblk(c, tb)
                init = HIN(c) if tb == 0 else col(CAR, c)
                P.op("dve", lambda e, ab=ab, bb=bb, init=init: e.tensor_tensor_scan(out=HS, data0=ab, data1=bb, initial=init, op0=ALU.mult, op1=ALU.add),
                     [ab, bb, init], [HS])
                if tb < NB - 1:
                    cc_ = col(CAR, c)
                    P.op("dve", lambda e, cc_=cc_: e.tensor_copy(out=cc_, in_=HS[:, 511:512]), [HS[:, 511:512]], [cc_])
                yl = Y[:, 4 + c, :]
                P.op("pool", lambda e, yl=yl, GG=GG: e.tensor_tensor(out=yl, in0=HS, in1=GG, op=ALU.mult), [HS, GG], [yl])
                return
            w = 2 ** (g + 1)
            UP = UPB[bi % 2]
            DBF = DBFS[bi % 2]
            ub = UP[:, HALO:HALO + 512]
            if tb > 0:
                tl = TAIL[:, g, :]
                P.op("dve", lambda e, UP=UP, tl=tl: e.tensor_copy(out=UP[:, 0:HALO], in_=tl), [tl], [UP[:, 0:HALO]])
            if tb < NB - 1:
                tl = TAIL[:, g, :]
                ue = UP[:, 512:512 + HALO]
                P.op("dve", lambda e, tl=tl, ue=ue: e.tensor_copy(out=tl, in_=ue), [ue], [tl])
            eng = "pool" if g >= 2 else "dve"
            bufs = SBP if g >= 2 else SB_
            src = UP
            lo = 0
            for k in range(g + 1):
                sh = 2 ** k
                dst = bufs[k % 2]
                lo2 = lo + sh
                o_ = dst[:, lo2:L]
                i0 = src[:, lo2:L]
                i1 = src[:, lo2 - sh:L - sh]
                P.op(eng, lambda e, o_=o_, i0=i0, i1=i1: e.tensor_tensor(out=o_, in0=i0, in1=i1, op=ALU.add), [i0, i1], [o_])
                src = dst
                lo = lo2
            s_ = src[:, HALO:L]
            P.op("dve", lambda e, s_=s_, ub=ub, w=w, DBF=DBF: e.scalar_tensor_tensor(out=DBF, in0=s_, scalar=1.0 / w, in1=ub, op0=ALU.mult, op1=ALU.subtract),
                 [s_, ub], [DBF])
            if tb == 0:
                s16 = src[:, HALO:2 * HALO]
                iv = INVC[:, g, :]
                u16 = UP[:, HALO:2 * HALO]
                P.op("dve", lambda e, s16=s16, iv=iv: e.tensor_tensor(out=T16, in0=s16, in1=iv, op=ALU.mult), [s16, iv], [T16])
                P.op("dve", lambda e, u16=u16, DBF=DBF: e.tensor_tensor(out=DBF[:, 0:HALO], in0=T16, in1=u16, op=ALU.subtract), [T16, u16], [DBF[:, 0:HALO]])
            pp_ = bankA()
            pw = PW[:, g, :]
            P.op("pe", lambda e, pp_=pp_, pw=pw, DBF=DBF: e.matmul(pp_, pw, DBF, start=True, stop=True), [pw, DBF], [pp_])
            yp = Y[:, g, :]
            P.op("dve", lambda e, yp=yp, pp_=pp_, g=g: e.tensor_scalar(out=yp, in0=pp_, scalar1=PB_(g), scalar2=PSC(g), op0=ALU.add, op1=ALU.mult),
                 [pp_, VEC], [yp])

        def c_outproj(tb):
            for j in range(4):
                tt = tb * 4 + j
                s = xr_next()
                P.dma("sp", XR[s], x_d[tt * 128:(tt + 1) * 128, :], f"xr{s}")
                for nh in range(2):
                    po = bankB()
                    for c8 in range(8):
                        l_ = Y[:, c8, j * 128:(j + 1) * 128]
                        r_ = WO[:, c8, nh * 512:(nh + 1) * 512]
                        P.op("pe", lambda e, l_=l_, r_=r_, po=po, c8=c8: e.matmul(po, l_, r_, start=(c8 == 0), stop=(c8 == 7)), [l_, r_], [po])
                    hd = H[:, tt, nh * 512:(nh + 1) * 512]
                    xs = XR[s][:, nh * 512:(nh + 1) * 512]
                    P.op("dve", lambda e, hd=hd, po=po, xs=xs: e.tensor_tensor(out=hd, in0=po, in1=xs, op=ALU.add), [po, xs], [hd])
                sumsq(H[:, tt, :], col(SS, tt), SQC)

        items = []
        npool = nlru = 0
        for tb in range(NB):
            for c in range(4):
                items.append(("pool", tb, c, npool))
                npool += 1
            for c in range(4):
                items.append(("lru", tb, c, nlru))
                nlru += 1
        c_s1(items[0])
        for i, it in enumerate(items):
            if i + 1 < len(items):
                c_s1(items[i + 1])
            if i == 4:
                emit_fold()
            c_s2(it)
            if i % 8 == 7:
                c_outproj(i // 8)
        load_wup(1)
        load_wdn(1)
    else:
        for tt in range(NT):
            P.dma("sp", H[:, tt, :], x_d[tt * 128:(tt + 1) * 128, :], f"xr{tt % 2}")
        load_wup(0)
        load_wdn(0)
        load_wup(1)
        load_wdn(1)
        pump(32)

    def dump_h():
        for tt in range(NT):
            P.dma("sp", out_d[tt * 128:(tt + 1) * 128, :], H[:, tt, :], f"od{tt % 2}", is_output=True)

    if stage == "CW":
        for rep in range(6):
            load_wup(1)
            load_wdn(1)
        stage = "C"
    if stage == "C":
        dump_h()
        P.emit()
        return nc, P, A

    A.cur = SCR
    ZB = [A.alloc([D], BF16) for _ in range(2)]
    SQ = A.alloc([D], BF16)
    RT = [A.alloc([512], F32) for _ in range(3)]
    HID = [A.alloc([8, 512], BF16) for _ in range(2)]
    P.dma("sp", GS, gbc_d[:, D:2 * D], "gs")

    def norm_all_to_zt(gain):
        for tt in range(NT):
            sumsq(H[:, tt, :], col(SS, tt), SQ)
        rstd_from(SS[:, 0:NT], RMS[:, 0:NT], RSTD[:, 0:NT])

    if MONLY:
        norm_all_to_zt(GS)
    else:
        rstd_from(SS[:, 0:NT], RMS[:, 0:NT], RSTD[:, 0:NT])
    for tt in range(NT):
        s = tt % 2
        pump(1)
        scale_to(ZB[s], H[:, tt, :], col(RSTD, tt), GS)
        transposes_to(ZB[s], ZT[:, :, tt * 128:(tt + 1) * 128])

    rti = [0]

    def mlp_up(g, tb):
        hid = HID[(g * NB + tb) % 2]
        for fc in range(8):
            pu = bankA()
            for kc in range(KC):
                l_ = WUP[g][:, kc, fc * 128:(fc + 1) * 128]
                r_ = ZT[:, kc, tb * 512:(tb + 1) * 512]
                P.op("pe", lambda e, l_=l_, r_=r_, pu=pu, kc=kc: e.matmul(pu, l_, r_, start=(kc == 0), stop=(kc == KC - 1)), [l_, r_], [pu])
            rt = RT[rti[0] % 3]
            rti[0] += 1
            P.op("act", lambda e, rt=rt, pu=pu: e.activation(out=rt, in_=pu, func=AF.Relu), [pu], [rt])
            hd = hid[:, fc, :]
            if SQ_ENG == "pool":
                P.op("pool", lambda e, hd=hd, rt=rt: e.tensor_tensor(out=hd, in0=rt, in1=rt, op=ALU.mult), [rt], [hd])
            elif SQ_ENG == "act":
                P.op("act", lambda e, hd=hd, rt=rt: e.activation(out=hd, in_=rt, func=AF.Square), [rt], [hd])
            else:
                P.op("dve", lambda e, hd=hd, rt=rt: e.tensor_tensor(out=hd, in0=rt, in1=rt, op=ALU.mult), [rt], [hd])

    def mlp_down(g, tb):
        hid = HID[(g * NB + tb) % 2]
        for j in range(4):
            tt = tb * 4 + j
            for nh in range(2):
                pd = bankB()
                for fc in range(8):
                    l_ = hid[:, fc, j * 128:(j + 1) * 128]
                    r_ = WDN[g][:, fc, nh * 512:(nh + 1) * 512]
                    P.op("pe", lambda e, l_=l_, r_=r_, pd=pd, fc=fc: e.matmul(pd, l_, r_, start=(fc == 0), stop=(fc == 7)), [l_, r_], [pd])
                hd = H[:, tt, nh * 512:(nh + 1) * 512]
                P.op("dve", lambda e, hd=hd, pd=pd: e.tensor_tensor(out=hd, in0=pd, in1=hd, op=ALU.add), [pd, hd], [hd])

    seq = [(g, tb) for g in range(4) for tb in range(NB)]
    if stage.startswith("Dg"):
        gl = [int(ch) for ch in stage[2:]]
        seq = [(g, tb) for g in gl for tb in range(NB)]
        for g in gl:
            if g >= 2:
                load_wup(g)
                load_wdn(g)
    elif stage[0] in "DM" and len(stage) > 1:
        seq = seq[:int(stage[1:])]
    if seq:
        mlp_up(*seq[0])
    for i, (g, tb) in enumerate(seq):
        if stage.startswith("Dg"):
            if i + 1 < len(seq):
                mlp_up(*seq[i + 1])
            mlp_down(g, tb)
            continue
        pump(4)
        if i + 1 < len(seq):
            mlp_up(*seq[i + 1])
        mlp_down(g, tb)
        if tb == NB - 1:
            if g + 2 < 4:
                load_wup(g + 2)
                load_wdn(g + 2)
            elif g == 2:
                WPG = SLOT[up_slot[2]].rearrange("p (k n) -> p k n", k=KC)
                WPP = SLOT[dn_slot[2]][:, 0:2048].rearrange("p (k n) -> p k n", k=2)
                if WMODE == "sw":
                    P.dma("pool", WPG, w_pg_d.rearrange("(k p) n -> p k n", p=128), f"w_s{up_slot[2]}")
                    P.dma("pool", WPP, w_pp_d.rearrange("(k p) n -> p k n", p=128), f"w_s{dn_slot[2]}")
                else:
                    enqueue(WPG, w_pg_d, KC)
                    enqueue(WPP, w_pp_d, 2)

    pump(len(WQ))
    if stage[0] in "DM":
        dump_h()
        P.emit()
        return nc, P, A

    A.cur = SCR
    o_zb = A.cur
    ZB = [A.alloc([D], BF16) for _ in range(2)]
    SQ = A.alloc([D], BF16)
    GF = A.alloc([D], F32)
    o_bpl = A.cur
    BPL = A.alloc([D], F32)
    o_z3 = A.cur
    Z3T = [A.alloc([KC, 128], BF16) for _ in range(3)]
    PT = [A.alloc([PLE], F32) for _ in range(2)]
    PBF = [A.alloc([PLE], BF16) for _ in range(2)]
    PTT = [A.alloc([2, 128], BF16) for _ in range(3)]
    GT = A.alloc([D], F32)
    GTS = [GT, XR[0]]
    P.dma("sp", GS, gbc_d[:, 2 * D:3 * D], "gs")
    P.dma("sp", GF, gbc_d[:, 3 * D:4 * D], "gf")
    P.dma("sp", BPL, gbc_d[:, 4 * D:5 * D], "bpl")
    norm_all_to_zt(GS)

    def e_stage1a(tt):
        s = tt % 2
        scale_to(ZB[s], H[:, tt, :], col(RSTD, tt), GS)
        P.dma("sp", PT[s], p_d[tt * 128:(tt + 1) * 128, :], f"pt{s}")
        P.op("pool", lambda e, s=s: e.tensor_copy(out=PBF[s], in_=PT[s]), [PT[s]], [PBF[s]])

    def e_stage1b(tt):
        s = tt % 2
        s3 = tt % 3
        transposes_to(ZB[s], Z3T[s3])
        transposes_to(PBF[s], PTT[s3], nchunk=2)

    def e_stage2(tt):
        s = tt % 3
        GTt = GTS[tt % 2]
        pgs, pqs = [], []
        for nh in range(2):
            pg = bankB()
            for kc in range(KC):
                l_ = Z3T[s][:, kc, :]
                r_ = WPG[:, kc, nh * 512:(nh + 1) * 512]
                P.op("pe", lambda e, l_=l_, r_=r_, pg=pg, kc=kc: e.matmul(pg, l_, r_, start=(kc == 0), stop=(kc == KC - 1)), [l_, r_], [pg])
            pgs.append(pg)
        for nh in range(2):
            gt = GTt[:, nh * 512:(nh + 1) * 512]
            bp = BPL[:, nh * 512:(nh + 1) * 512]
            pg = pgs[nh]
            P.op("dve", lambda e, gt=gt, pg=pg, bp=bp: e.tensor_tensor(out=gt, in0=pg, in1=bp, op=ALU.add), [pg, bp], [gt])
        pq = bankA()
        for nh in range(2):
            pqn = pq if nh == 0 else bankA()
            for k2 in range(2):
                l_ = PTT[s][:, k2, :]
                r_ = WPP[:, k2, nh * 512:(nh + 1) * 512]
                P.op("pe", lambda e, l_=l_, r_=r_, pqn=pqn, k2=k2: e.matmul(pqn, l_, r_, start=(k2 == 0), stop=(k2 == 1)), [l_, r_], [pqn])
            pqs.append(pqn)
        for nh in range(2):
            gt = GTt[:, nh * 512:(nh + 1) * 512]
            P.op("act", lambda e, gt=gt: e.activation(out=gt, in_=gt, func=AF.Sigmoid), [gt], [gt])
        for nh in range(2):
            gt = GTt[:, nh * 512:(nh + 1) * 512]
            pqn = pqs[nh]
            P.op("dve", lambda e, pqn=pqn, gt=gt: e.tensor_tensor(out=gt, in0=pqn, in1=gt, op=ALU.mult), [pqn, gt], [gt])
        for nh in range(2):
            gt = GTt[:, nh * 512:(nh + 1) * 512]
            hd = H[:, tt, nh * 512:(nh + 1) * 512]
            P.op("pool", lambda e, hd=hd, gt=gt: e.tensor_tensor(out=hd, in0=hd, in1=gt, op=ALU.add), [hd, gt], [hd])
        if tt > 0:
            sumsq(H[:, tt - 1, :], col(SS, 16 + tt - 1), SQ)

    e_stage1a(0)
    e_stage1b(0)
    e_stage1a(1)
    e_stage1b(1)
    e_stage1a(2)
    for tt in range(NT):
        if tt + 3 < NT:
            e_stage1a(tt + 3)
        e_stage2(tt)
        if tt + 2 < NT:
            e_stage1b(tt + 2)
    sumsq(H[:, NT - 1, :], col(SS, 16 + NT - 1), SQ)
    rstd_from(SS[:, 16:16 + NT], RMS[:, 16:16 + NT], RSTD[:, 16:16 + NT])
    OUTS = [XR[0], XR[1], A.alloc([D], F32, at=o_zb), A.alloc([D], F32, at=o_bpl), A.alloc([D], F32, at=o_z3)]
    for tt in range(NT):
        so = OUTS[tt % len(OUTS)]
        if tt % 3 != 2:
            scale_to(so, H[:, tt, :], col(RSTD, 16 + tt), GF)
        else:
            hsrc = H[:, tt, :]
            rc = col(RSTD, 16 + tt)
            P.op("act", lambda e, hsrc=hsrc, rc=rc: e.activation(out=GT, in_=hsrc, func=AF.Copy, scale=rc), [hsrc, rc], [GT])
            P.op("pool", lambda e, so=so: e.tensor_tensor(out=so, in0=GT, in1=GF, op=ALU.mult), [GT, GF], [so])
        P.dma("sp", out_d[tt * 128:(tt + 1) * 128, :], so, f"ox{tt % len(OUTS)}", is_output=True)
    P.emit()
    return nc, P, A


def make_in_maps(x, p, norm_mix_g, w_in, pool_w, pool_b, pool_scale, conv_w, conv_b,
                 gate_a_w, gate_a_b, gate_x_w, gate_x_b, lru_L, w_out, norm_mlp_g,
                 w_up, w_down, norm_ple_g, w_ple_gate, b_ple_gate, w_ple_proj, norm_final_g):
    f = lambda a: np.ascontiguousarray(np.asarray(a, dtype=np.float32))
    x = f(x)
    p = f(p)
    B, S, _ = x.shape
    nchunk = NCORES // B

    def bd(wg):
        wg = f(wg)[0]
        o = np.zeros((4, 128, 128), np.float32)
        for c in range(4):
            o[c, 0:64, 0:64] = wg[2 * c]
            o[c, 64:128, 64:128] = wg[2 * c + 1]
        return o

    def pc(v, n):
        return f(v).reshape(n, 128).T

    vecs = np.zeros((128, 40), np.float32)
    vecs[:, 0:4] = pc(f(pool_b)[0].reshape(-1), 4)
    vecs[:, 4:8] = pc(f(pool_scale)[0], 4)
    cw = f(conv_w)[0]
    for c in range(4):
        for k in range(4):
            vecs[:, 8 + c * 4 + k] = cw[k, c * 128:(c + 1) * 128]
    vecs[:, 24:28] = pc(f(conv_b)[0], 4)
    vecs[:, 28:32] = pc(f(gate_a_b)[0].reshape(-1), 4)
    vecs[:, 32:36] = pc(f(gate_x_b)[0].reshape(-1), 4)
    vecs[:, 36:40] = pc(f(lru_L)[0], 4)
    gbc = np.concatenate([f(norm_mix_g)[0], f(norm_mlp_g)[0], f(norm_ple_g)[0], f(norm_final_g), f(b_ple_gate)[0]])
    gbc = np.ascontiguousarray(np.broadcast_to(gbc[None, :], (128, 5 * D)))
    shared = {
        "w_in": f(w_in)[0], "w_out": f(w_out)[0], "w_up": f(w_up)[0], "w_down": f(w_down)[0],
        "w_pg": f(w_ple_gate)[0], "w_pp": f(w_ple_proj)[0], "pool_w": f(pool_w)[0],
        "gaw": bd(gate_a_w), "gxw": bd(gate_x_w), "vecs": vecs, "gbc": gbc,
        "ident": np.eye(128, dtype=np.float32),
    }
    maps = []
    for r in range(NCORES):
        b, j = r // nchunk, r % nchunk
        lo = j * T
        xs = x[b, lo:lo + T]
        xh = x[b, lo - HALO:lo] if j > 0 else np.zeros((HALO, D), np.float32)
        cm = np.zeros((128, 66), np.float32)
        first = 1.0 if j == 0 else 0.0
        cm[:, 0] = first
        cm[:, 1] = 1.0 - first
        for k in range(CCG):
            kg = (r // CCG) * CCG + k
            m = 1.0 if (kg // nchunk == b and kg % nchunk < j) else 0.0
            cm[:, 2 + k * 4:2 + k * 4 + 4] = m
            cm[:, 34 + k * 4:34 + k * 4 + 4] = 1.0 - m
        ic = np.zeros((128, 64), np.float32)
        for g in range(4):
            w = 2 ** (g + 1)
            for t in range(16):
                ic[:, g * 16 + t] = 1.0 / min(lo + t + 1, w)
        m = dict(shared)
        m.update({"x": np.ascontiguousarray(xs), "xh": np.ascontiguousarray(xh),
                  "p": np.ascontiguousarray(p[0, b, lo:lo + T]), "cmask": cm, "invc": ic})
        maps.append(m)
    return maps, B, nchunk


_NC_CACHE = {}


def kernel(**inputs):
    maps, B, nchunk = make_in_maps(**inputs)
    if "full" not in _NC_CACHE:
        _NC_CACHE["full"] = build_nc("full")[0]
    nc = _NC_CACHE["full"]
    res = run_bass_kernel_spmd(nc, maps, core_ids=list(range(NCORES)))
    out = np.empty((B, nchunk * T, D), np.float32)
    for r in range(NCORES):
        b, j = r // nchunk, r % nchunk
        out[b, j * T:(j + 1) * T] = res.results[r]["out"]
    return out
```

```python
import numpy as np
import concourse.bass as bass
import concourse.mybir as mybir
from concourse.bass_utils import run_bass_kernel_spmd

F32 = mybir.dt.float32
BF16 = mybir.dt.bfloat16
AF = mybir.ActivationFunctionType
ALU = mybir.AluOpType
AX = mybir.AxisListType
DSZ = {F32: 4, BF16: 2}

NCORES = 8
T = 2048
D = 1024
NT = 16
NB = 4
KC = 8
DFF = 4096
PLE = 256
EPS = 1e-6
HALO = 16
SQ_ENG = "act"
import os
NOCC = bool(os.environ.get("NOCC"))
WMODE = os.environ.get("WMODE", "sw")
CCG = int(os.environ.get("CCG", "4"))


class Op:
    __slots__ = ("eng", "fn", "deps", "sem", "val", "signal", "is_dma", "inc", "name", "idx", "key")

    def __init__(self, eng, fn, is_dma=False, name=""):
        self.eng = eng
        self.fn = fn
        self.deps = []
        self.sem = None
        self.val = None
        self.signal = False
        self.is_dma = is_dma
        self.inc = 1
        self.name = name
        self.idx = -1
        self.key = None


def ap_intervals(ap, is_dram=False, cap=64):
    dsz = DSZ[ap.dtype]
    dims = list(ap.ap)
    off = int(ap.offset)
    if not is_dram:
        pstep = dims[0][0]
        dims = dims[1:]
        if pstep > 0:
            off = off % pstep
    dims = [(s, c) for (s, c) in dims if c > 1]
    run = 1
    while dims and dims[-1][0] == run:
        run *= dims[-1][1]
        dims.pop()
    n_outer = 1
    for s, c in dims:
        n_outer *= c
    if n_outer > cap or any(s < 0 for s, c in dims):
        lo = off + sum(min(0, s * (c - 1)) for s, c in dims)
        hi = off + sum(max(0, s * (c - 1)) for s, c in dims) + run
        return [(lo * dsz, hi * dsz)]
    offs = [off]
    for s, c in dims:
        offs = [o + s * i for o in offs for i in range(c)]
    return [(o * dsz, (o + run) * dsz) for o in offs]


class Prog:
    ENGS = ["pe", "act", "dve", "pool", "sp"]

    def __init__(self, nc, same_engine_sync=True):
        self.nc = nc
        self.ops = {e: [] for e in self.ENGS}
        self.hist = {}
        self.dma_keys = {}
        self.eng_sem = {}
        self.same_engine_sync = same_engine_sync
        self.out_dma_ops = []
        self._sem_ctx = []

    def _new_sem(self, name):
        cm = self.nc.semaphore(name)
        s = cm.__enter__()
        self._sem_ctx.append(cm)
        return s

    def _regions(self, aps):
        out = []
        for a in aps:
            if a is None:
                continue
            space = str(a.space)
            is_dram = "dram" in space.lower()
            out.append((a.tensor.name, ap_intervals(a, is_dram=is_dram)))
        return out

    def _add_deps(self, op, reads, writes):
        deps = []
        rregs = self._regions(reads)
        wregs = self._regions(writes)
        for key, ivs in rregs:
            h = self.hist.setdefault(key, [])
            for lo, hi in ivs:
                for e in h:
                    if e[2] and e[0] < hi and lo < e[1]:
                        deps.append(e[3])
        for key, ivs in wregs:
            h = self.hist.setdefault(key, [])
            for lo, hi in ivs:
                for e in h:
                    if e[0] < hi and lo < e[1]:
                        deps.append(e[3])
        for key, ivs in rregs:
            h = self.hist[key]
            for lo, hi in ivs:
                h.append([lo, hi, False, op])
        for key, ivs in wregs:
            h = self.hist[key]
            for lo, hi in ivs:
                h[:] = [e for e in h if not (lo <= e[0] and e[1] <= hi)]
                h.append([lo, hi, True, op])
        best = {}
        for d in deps:
            if d is op:
                continue
            if (not d.is_dma) and d.eng == op.eng and (not op.is_dma):
                if op.eng == "pe" or not self.same_engine_sync:
                    continue
            k = ("dma", d.key) if d.is_dma else ("eng", d.eng)
            if k not in best or d.idx > best[k].idx:
                best[k] = d
        have = {id(d) for d in op.deps}
        for d in best.values():
            if id(d) not in have:
                op.deps.append(d)

    def op(self, eng, fn, reads=(), writes=(), name=""):
        o = Op(eng, fn, name=name)
        o.idx = len(self.ops[eng])
        self._add_deps(o, reads, writes)
        self.ops[eng].append(o)
        return o

    def wait_for(self, eng, ops):
        o = Op(eng, None, name="waitfor")
        o.idx = len(self.ops[eng])
        o.deps = [d for d in ops if d is not None]
        self.ops[eng].append(o)
        return o

    def dma(self, queue, out, in_, key, is_output=False, fn=None, inc=16):
        if fn is None:
            fn = lambda e, out=out, in_=in_: e.dma_start(out=out, in_=in_)
        o = Op(queue, fn, is_dma=True, name="dma:" + key)
        if key not in self.dma_keys:
            self.dma_keys[key] = [self._new_sem("d_" + key), 0, None]
        k = self.dma_keys[key]
        o.key = key
        o.idx = k[1]
        if k[2] is not None:
            o.deps.append(k[2])
        self._add_deps(o, [in_], [out])
        k[1] += inc
        k[2] = o
        o.sem = k[0]
        o.val = k[1]
        o.inc = inc
        o.signal = True
        self.ops[queue].append(o)
        if is_output:
            self.out_dma_ops.append(o)
        return o

    def emit(self):
        nc = self.nc
        for e in self.ENGS:
            self.eng_sem[e] = self._new_sem("e_" + e)
        fin = Op("sp", None, name="final")
        fin.deps = list(self.out_dma_ops) + [k[2] for k in self.dma_keys.values() if k[2] is not None]
        self.ops["sp"].append(fin)
        for e in self.ENGS:
            for o in self.ops[e]:
                for d in o.deps:
                    if not d.is_dma:
                        d.signal = True
        for e in self.ENGS:
            cnt = 0
            for o in self.ops[e]:
                if o.is_dma:
                    continue
                if o.signal:
                    cnt += 1
                    o.sem = self.eng_sem[e]
                    o.val = cnt
        stats = {}

        def run(ename, eng):
            waited = {}
            nw = 0
            for o in self.ops[ename]:
                need = {}
                for d in o.deps:
                    s = d.sem
                    if d.val > need.get(id(s), (None, 0))[1]:
                        need[id(s)] = (s, d.val)
                for sid, (s, v) in need.items():
                    if waited.get(sid, 0) >= v:
                        continue
                    eng.wait_ge(s, v)
                    waited[sid] = v
                    nw += 1
                if o.fn is None:
                    continue
                ins = o.fn(eng)
                if o.signal:
                    ins.then_inc(o.sem, o.inc)
            stats[ename] = (len(self.ops[ename]), nw)

        with nc.Block() as block:

            @block.tensor
            def _(e):
                run("pe", e)

            @block.scalar
            def _(e):
                run("act", e)

            @block.vector
            def _(e):
                run("dve", e)

            @block.gpsimd
            def _(e):
                run("pool", e)

            @block.sync
            def _(e):
                run("sp", e)

        for cm in reversed(self._sem_ctx):
            cm.__exit__(None, None, None)
        self._sem_ctx = []
        self.stats = stats
        return stats


class Arena:
    def __init__(self, nc, nbytes):
        self.nc = nc
        self.nbytes = nbytes
        self.t = nc.alloc_sbuf_tensor("arena", [128, nbytes // 4], F32)
        self.cur = 0
        self.peak = 0

    def alloc(self, free_shape, dtype, at=None):
        ne = int(np.prod(free_shape))
        n = ne * DSZ[dtype]
        n = (n + 63) // 64 * 64
        if at is None:
            at = self.cur
            self.cur += n
        assert at % 4 == 0
        assert at + n <= self.nbytes, f"arena overflow {at + n} > {self.nbytes}"
        self.peak = max(self.peak, at + n)
        v = self.t[:, at // 4:(at + n) // 4]
        if dtype != F32:
            v = v.bitcast(dtype)
        v = v[:, 0:ne]
        if len(free_shape) == 2:
            v = v.rearrange("p (a b) -> p a b", a=free_shape[0])
        elif len(free_shape) == 3:
            v = v.rearrange("p (a b c) -> p a b c", a=free_shape[0], b=free_shape[1])
        return v


def build_nc(stage="full"):
    nc = bass.Bass("TRN2", target_bir_lowering=False)

    def din(name, shape):
        return nc.dram_tensor(name, shape, F32, kind="ExternalInput").ap()

    x_d = din("x", [T, D])
    xh_d = din("xh", [HALO, D])
    p_d = din("p", [T, PLE])
    w_in_d = din("w_in", [D, 1536])
    w_out_d = din("w_out", [D, D])
    w_up_d = din("w_up", [D, DFF])
    w_dn_d = din("w_down", [DFF, D])
    w_pg_d = din("w_pg", [D, D])
    w_pp_d = din("w_pp", [PLE, D])
    pw_d = din("pool_w", [4, 128, 128])
    gaw_d = din("gaw", [4, 128, 128])
    gxw_d = din("gxw", [4, 128, 128])
    vec_d = din("vecs", [128, 40])
    gbc_d = din("gbc", [128, 5 * D])
    id_d = din("ident", [128, 128])
    cm_d = din("cmask", [128, 66])
    ic_d = din("invc", [128, 64])
    out_d = nc.dram_tensor("out", [T, D], F32, kind="ExternalOutput").ap()
    cc_in = nc.dram_tensor("cc_in", [128, 8], F32)
    cc_out = nc.dram_tensor("cc_out", [CCG * 128, 8], F32)

    P = Prog(nc)
    A = Arena(nc, 212736)

    HREG = A.alloc([NT * D], F32)
    H = HREG.rearrange("p (t d) -> p t d", t=NT)
    ZT = A.alloc([KC, T], BF16)
    SLOT = [A.alloc([8192], BF16) for _ in range(4)]
    XR = [A.alloc([D], F32) for _ in range(2)]
    GS = A.alloc([D], F32)
    VEC = A.alloc([40], F32)
    DV = A.alloc([24], F32)
    CM = A.alloc([66], F32)
    INVC = A.alloc([4, 16], F32)
    IDB = A.alloc([128], BF16)
    PW = A.alloc([4, 128], BF16)
    GAW = A.alloc([4, 128], BF16)
    GXW = A.alloc([4, 128], BF16)
    SS = A.alloc([32], F32)
    RMS = A.alloc([32], F32)
    RSTD = A.alloc([32], F32)
    RS = A.alloc([16], F32)
    RSS = A.alloc([4], F32)
    SUMM = A.alloc([8], F32)
    GATH = A.alloc([CCG, 8], F32)
    APR = A.alloc([CCG, 4], F32)
    HPR = A.alloc([CCG, 4], F32)
    FOLD = A.alloc([4, CCG], F32)
    CAR = A.alloc([4], F32)
    TAIL = A.alloc([4, 16], F32)
    ZTH = A.alloc([KC, HALO], BF16)
    E12 = A.alloc([8], F32)
    EPSC = A.alloc([1], F32)
    SCR = A.cur
    IDF = A.alloc([128], F32, at=SCR + 8192)
    A.peak_persist = SCR

    ps = [nc.alloc_psum_tensor(f"ps{i}", [128, 512], F32) for i in range(8)]
    TRB = [ps[i][:, :].bitcast(BF16) for i in range(2)]
    rr = {"A": 0, "B": 0, "T": 0}

    def bankA():
        b = ps[2 + rr["A"] % 3][:, :]
        rr["A"] += 1
        return b

    def bankB():
        b = ps[5 + rr["B"] % 3][:, :]
        rr["B"] += 1
        return b

    def bankT():
        b = TRB[rr["T"] % 2]
        rr["T"] += 1
        return b

    def col(ap, i):
        return ap[:, i:i + 1]

    P.dma("sp", VEC, vec_d, "c_vec")
    P.dma("sp", CM, cm_d, "c_cm")
    P.dma("sp", INVC, ic_d.rearrange("p (g t) -> p g t", g=4), "c_ic")
    P.dma("sp", IDF, id_d, "c_id")
    P.dma("sp", GS, gbc_d[:, 0:D], "gs")
    P.op("pool", lambda e: e.tensor_copy(out=IDB, in_=IDF), [IDF], [IDB])
    WIL = SLOT[0][:, 0:4096].rearrange("p (k n) -> p k n", k=KC)
    WIP = SLOT[1][:, 0:4096].rearrange("p (k n) -> p k n", k=KC)
    WIG = SLOT[1][:, 4096:8192].rearrange("p (k n) -> p k n", k=KC)
    P.dma("pool", WIL, w_in_d[:, 512:1024].rearrange("(k p) n -> p k n", p=128), "w_s0")
    P.dma("pool", GAW, gaw_d.rearrange("c k m -> k c m"), "c_gaw")
    P.dma("pool", GXW, gxw_d.rearrange("c k m -> k c m"), "c_gxw")

    LCOL = VEC[:, 36:40]
    P.op("act", lambda e: e.activation(out=E12[:, 0:4], in_=LCOL, func=AF.Exp, scale=-1.0), [LCOL], [E12[:, 0:4]])
    P.op("act", lambda e: e.activation(out=E12[:, 4:8], in_=E12[:, 0:4], func=AF.Ln, bias=1.0), [E12[:, 0:4]], [E12[:, 4:8]])
    SP_ = E12[:, 4:8]
    P.op("dve", lambda e: e.tensor_scalar(out=DV[:, 8:12], in0=SP_, scalar1=-8.0, scalar2=None, op0=ALU.mult), [SP_], [DV[:, 8:12]])
    P.op("dve", lambda e: e.tensor_scalar(out=DV[:, 12:16], in0=SP_, scalar1=-4.0, scalar2=None, op0=ALU.mult), [SP_], [DV[:, 12:16]])
    P.op("dve", lambda e: e.tensor_scalar(out=DV[:, 16:20], in0=SP_, scalar1=-8192.0, scalar2=None, op0=ALU.mult), [SP_], [DV[:, 16:20]])
    P.op("dve", lambda e: e.tensor_scalar(out=DV[:, 0:4], in0=VEC[:, 28:32], scalar1=0.5, scalar2=None, op0=ALU.mult), [VEC[:, 28:32]], [DV[:, 0:4]])
    P.op("dve", lambda e: e.tensor_scalar(out=DV[:, 4:8], in0=VEC[:, 32:36], scalar1=0.5, scalar2=None, op0=ALU.mult), [VEC[:, 32:36]], [DV[:, 4:8]])
    HBA = lambda c: col(DV, 0 + c)
    HBX = lambda c: col(DV, 4 + c)
    C1 = lambda c: col(DV, 8 + c)
    HC1 = lambda c: col(DV, 12 + c)
    KC1 = lambda c: col(DV, 16 + c)
    PB_ = lambda g: col(VEC, 0 + g)
    PSC = lambda g: col(VEC, 4 + g)
    CW = lambda c, k: col(VEC, 8 + c * 4 + k)
    CB = lambda c: col(VEC, 24 + c)

    def transposes_to(zb, dst3, npart=128, ncols=128, nchunk=KC):
        tb_ = bankT()
        for kc in range(nchunk):
            o = tb_[:, kc * ncols:(kc + 1) * ncols]
            i = zb[0:npart, kc * 128:(kc + 1) * 128]
            idn = IDB[0:npart, 0:npart]
            P.op("pe", lambda e, o=o, i=i, idn=idn: e.transpose(o, i, idn), [i, idn], [o])
        src = tb_[:, 0:nchunk * ncols].rearrange("p (k n) -> p k n", k=nchunk)
        P.op("act", lambda e, src=src: e.copy(out=dst3, in_=src), [src], [dst3])

    def sumsq(src, sscol, SQ, npart=128):
        s_ = src[0:npart]
        q_ = SQ[0:npart]
        c_ = sscol[0:npart]
        P.op("act", lambda e: e.activation(out=q_, in_=s_, func=AF.Square, accum_out=c_), [s_], [q_, c_])

    def rstd_from(sscols, rmscols, rstdcols, npart=128):
        a, b, c = sscols[0:npart], rmscols[0:npart], rstdcols[0:npart]
        P.op("act", lambda e: e.activation(out=b, in_=a, func=AF.Sqrt, scale=1.0 / D, bias=EPSC[0:npart]), [a, EPSC], [b])
        P.op("dve", lambda e: e.reciprocal(out=c, in_=b), [b], [c])

    def scale_to(dst, src, rstdcol, gain, npart=128):
        d_, s_, r_, g_ = dst[0:npart], src[0:npart], rstdcol[0:npart], gain[0:npart]
        P.op("dve", lambda e: e.scalar_tensor_tensor(out=d_, in0=s_, scalar=r_, in1=g_, op0=ALU.mult, op1=ALU.mult),
             [s_, r_, g_], [d_])

    P.op("pool", lambda e: e.memset(EPSC, EPS), [], [EPSC])

    WUP = [None] * 4
    WDN = [None] * 4
    up_slot = {0: 3, 1: 1, 2: 3, 3: 1}
    dn_slot = {0: 0, 1: 2, 2: 0, 3: 2}

    xri = [0]

    def xr_next():
        k = xri[0] % 2
        xri[0] += 1
        return k

    WQ = []

    def enqueue(dst3, src2d, nk):
        for k in range(nk):
            WQ.append((dst3[:, k, :], src2d[k * 128:(k + 1) * 128, :]))

    def pump(n):
        for _ in range(min(n, len(WQ))):
            dst, src = WQ.pop(0)
            sl = xr_next()
            P.dma("sp", XR[sl], src, f"xr{sl}")
            P.op("pool", lambda e, dst=dst, sl=sl: e.tensor_copy(out=dst, in_=XR[sl]), [XR[sl]], [dst])

    def load_wup(g):
        v = SLOT[up_slot[g]].rearrange("p (k n) -> p k n", k=KC)
        if WMODE == "sw":
            P.dma("pool", v, w_up_d[:, g * 1024:(g + 1) * 1024].rearrange("(k p) n -> p k n", p=128), f"w_s{up_slot[g]}")
        else:
            enqueue(v, w_up_d[:, g * 1024:(g + 1) * 1024], KC)
        WUP[g] = v

    def load_wdn(g):
        v = SLOT[dn_slot[g]].rearrange("p (k n) -> p k n", k=KC)
        if WMODE == "sw":
            P.dma("pool", v, w_dn_d[g * 1024:(g + 1) * 1024, :].rearrange("(k p) n -> p k n", p=128), f"w_s{dn_slot[g]}")
        else:
            enqueue(v, w_dn_d[g * 1024:(g + 1) * 1024, :], KC)
        WDN[g] = v

    MONLY = stage.startswith("M")
    if not MONLY:
        A.cur = SCR
        ZB = [A.alloc([D], BF16) for _ in range(2)]
        SQ = A.alloc([D], BF16)
        sl = xr_next()
        XH = XR[sl]
        P.dma("sp", XH[0:HALO], xh_d, f"xr{sl}")
        sumsq(XH, col(SS, 16), SQ, HALO)
        rstd_from(col(SS, 16), col(RMS, 16), col(RSTD, 16), HALO)
        scale_to(ZB[1], XH, col(RSTD, 16), GS, HALO)
        transposes_to(ZB[1], ZTH, npart=HALO, ncols=HALO)
        XA = [A.alloc([D], F32) for _ in range(5)] + [XR[0], XR[1]]
        NXA = len(XA)

        def a_load(tt):
            P.dma("sp", XA[tt % NXA], x_d[tt * 128:(tt + 1) * 128, :], f"xa{tt % NXA}")

        def a_stats_act(tt):
            sumsq(XA[tt % NXA], col(SS, tt), SQ)
            a_, b_ = col(SS, tt), col(RMS, tt)
            P.op("act", lambda e, a_=a_, b_=b_: e.activation(out=b_, in_=a_, func=AF.Sqrt, scale=1.0 / D, bias=EPSC), [a_, EPSC], [b_])

        def a_recip(tt):
            b_, c_ = col(RMS, tt), col(RSTD, tt)
            P.op("dve", lambda e, b_=b_, c_=c_: e.reciprocal(out=c_, in_=b_), [b_], [c_])

        def a_stage2(tt):
            s = tt % 2
            scale_to(ZB[s], XA[tt % NXA], col(RSTD, tt), GS)
            transposes_to(ZB[s], ZT[:, :, tt * 128:(tt + 1) * 128])

        for tt in range(NXA - 1):
            a_load(tt)
        for tt in range(2):
            a_stats_act(tt)
            a_recip(tt)
        for tt in range(NT):
            if tt + 2 < NT:
                a_stats_act(tt + 2)
            a_stage2(tt)
            if tt + 2 < NT:
                a_recip(tt + 2)
            if tt + NXA - 1 < NT:
                a_load(tt + NXA - 1)

        P.dma("pool", WIP, w_in_d[:, 0:512].rearrange("(k p) n -> p k n", p=128), "w_s1a")
        P.dma("pool", WIG, w_in_d[:, 1024:1536].rearrange("(k p) n -> p k n", p=128), "w_s1b")
        P.dma("pool", PW, pw_d.rearrange("g c d -> c g d"), "c_pw")
        A.cur = SCR
        XB = [A.alloc([512], F32) for _ in range(2)]
        XBB = [A.alloc([512], BF16) for _ in range(2)]
        THR = [A.alloc([512], F32) for _ in range(2)]
        THI = [A.alloc([512], F32) for _ in range(2)]
        SC1 = [A.alloc([512], F32) for _ in range(2)]
        UCs = [SLOT[2].bitcast(F32)[:, 0:HALO + T], SLOT[3].bitcast(F32)[:, 0:HALO + T]]
        HF = HREG

        def Ablk(c, tb):
            return HF[:, tb * 4096 + c * 512: tb * 4096 + (c + 1) * 512]

        def Bblk(c, tb):
            return HF[:, tb * 4096 + 2048 + c * 512: tb * 4096 + 2048 + (c + 1) * 512]

        def b_s1(k):
            c, tb = k // 4, k % 4
            UC = UCs[c % 2]
            if tb == 0:
                pz = bankA()
                for kc in range(KC):
                    l_ = WIL[:, kc, c * 128:(c + 1) * 128]
                    r_ = ZTH[:, kc, :]
                    P.op("pe", lambda e, l_=l_, r_=r_, pz=pz, kc=kc: e.matmul(pz[:, 0:HALO], l_, r_, start=(kc == 0), stop=(kc == KC - 1)),
                         [l_, r_], [pz[:, 0:HALO]])
                P.op("act", lambda e, pz=pz, UC=UC: e.copy(out=UC[:, 0:HALO], in_=pz[:, 0:HALO]), [pz[:, 0:HALO]], [UC[:, 0:HALO]])
            base = HALO + tb * 512
            pz = bankA()
            for kc in range(KC):
                l_ = WIL[:, kc, c * 128:(c + 1) * 128]
                r_ = ZT[:, kc, tb * 512:(tb + 1) * 512]
                P.op("pe", lambda e, l_=l_, r_=r_, pz=pz, kc=kc: e.matmul(pz, l_, r_, start=(kc == 0), stop=(kc == KC - 1)), [l_, r_], [pz])
            ub = UC[:, base:base + 512]
            P.op("act", lambda e, pz=pz, ub=ub: e.copy(out=ub, in_=pz), [pz], [ub])

        def b_s2a(k):
            c, tb = k // 4, k % 4
            UC = UCs[c % 2]
            q = k % 2
            base = HALO + tb * 512
            xb = XB[q]
            u0 = UC[:, base - 3:base - 3 + 512]
            P.op("dve", lambda e, xb=xb, u0=u0, c=c: e.tensor_scalar(out=xb, in0=u0, scalar1=CW(c, 0), scalar2=CB(c), op0=ALU.mult, op1=ALU.add),
                 [u0, VEC], [xb])
            for kk in range(1, 4):
                uk = UC[:, base - 3 + kk:base - 3 + kk + 512]
                P.op("dve", lambda e, xb=xb, uk=uk, c=c, kk=kk: e.scalar_tensor_tensor(out=xb, in0=uk, scalar=CW(c, kk), in1=xb, op0=ALU.mult, op1=ALU.add),
                     [uk, xb, VEC], [xb])
            xbb = XBB[q]
            P.op("act", lambda e, xbb=xbb, xb=xb: e.copy(out=xbb, in_=xb), [xb], [xbb])
            pr = bankA()
            ga = GAW[:, c, :]
            P.op("pe", lambda e, pr=pr, ga=ga, xbb=xbb: e.matmul(pr, ga, xbb, start=True, stop=True), [ga, xbb], [pr])
            thr = THR[q]
            rsc = col(RS, c * 4 + tb)
            P.op("act", lambda e, thr=thr, pr=pr, c=c, rsc=rsc: e.activation(out=thr, in_=pr, func=AF.Tanh, scale=0.5, bias=HBA(c), accum_out=rsc),
                 [pr, DV], [thr, rsc])
            pi = bankA()
            gx = GXW[:, c, :]
            P.op("pe", lambda e, pi=pi, gx=gx, xbb=xbb: e.matmul(pi, gx, xbb, start=True, stop=True), [gx, xbb], [pi])
            thi = THI[q]
            P.op("act", lambda e, thi=thi, pi=pi, c=c: e.activation(out=thi, in_=pi, func=AF.Tanh, scale=0.5, bias=HBX(c)), [pi, DV], [thi])
            ab, bb = Ablk(c, tb), Bblk(c, tb)
            P.op("act", lambda e, ab=ab, thr=thr, c=c: e.activation(out=ab, in_=thr, func=AF.Exp, scale=HC1(c), bias=HC1(c)), [thr, DV], [ab])
            P.op("act", lambda e, bb=bb, thr=thr, c=c: e.activation(out=bb, in_=thr, func=AF.Exp, scale=C1(c), bias=C1(c)), [thr, DV], [bb])

        def b_s2b(k):
            c, tb = k // 4, k % 4
            UC = UCs[c % 2]
            q = k % 2
            thi, xb = THI[q], XB[q]
            qb = UC[:, tb * 512:(tb + 1) * 512]
            P.op("pool", lambda e, thi=thi: e.tensor_scalar(out=thi, in0=thi, scalar1=1.0, scalar2=0.5, op0=ALU.add, op1=ALU.mult), [thi], [thi])
            P.op("pool", lambda e, qb=qb, thi=thi, xb=xb: e.tensor_tensor(out=qb, in0=thi, in1=xb, op=ALU.mult), [thi, xb], [qb])

        prevs = {}

        def b_tail_act(c):
            for tb in range(NB):
                bb = Bblk(c, tb)
                P.op("act", lambda e, bb=bb: e.activation(out=bb, in_=bb, func=AF.Sqrt, scale=-1.0, bias=1.0), [bb], [bb])
            b0 = Bblk(c, 0)[:, 0:1]
            P.op("dve", lambda e, b0=b0: e.tensor_scalar(out=b0, in0=b0, scalar1=col(CM, 1), scalar2=col(CM, 0), op0=ALU.mult, op1=ALU.add),
                 [b0, CM], [b0])

        def b_tail_step(c, tb):
            UC = UCs[c % 2]
            prev = prevs.get(c)
            bb = Bblk(c, tb)
            ab = Ablk(c, tb)
            qb = UC[:, tb * 512:(tb + 1) * 512]
            P.op("pool", lambda e, bb=bb, qb=qb: e.tensor_tensor(out=bb, in0=qb, in1=bb, op=ALU.mult), [qb, bb], [bb])
            so = SC1[tb % 2]
            init = 0.0 if prev is None else prev[:, 511:512]
            rd = [ab, bb] + ([] if prev is None else [prev[:, 511:512]])
            P.op("dve", lambda e, so=so, ab=ab, bb=bb, init=init: e.tensor_tensor_scan(out=so, data0=ab, data1=bb, initial=init, op0=ALU.mult, op1=ALU.add),
                 rd, [so])
            prevs[c] = so
            if tb == NB - 1:
                hl = col(SUMM, 4 + c)
                P.op("dve", lambda e, hl=hl, so=so: e.tensor_copy(out=hl, in_=so[:, 511:512]), [so[:, 511:512]], [hl])
                rs4 = RS[:, c * 4:(c + 1) * 4]
                rss = col(RSS, c)
                P.op("dve", lambda e, rs4=rs4, rss=rss: e.reduce_sum(out=rss, in_=rs4, axis=AX.X), [rs4], [rss])
                at = col(SUMM, c)
                P.op("act", lambda e, at=at, rss=rss, c=c: e.activation(out=at, in_=rss, func=AF.Exp, scale=HC1(c), bias=KC1(c)), [rss, DV], [at])

        b_s1(0)
        b_s1(1)
        b_s2a(0)
        for k in range(16):
            if k + 2 < 16:
                b_s1(k + 2)
            if k + 1 < 16:
                b_s2a(k + 1)
            b_s2b(k)
            if k >= 4:
                b_tail_step(k // 4 - 1, k % 4)
            if k % 4 == 3:
                b_tail_act(k // 4)
        for tb in range(NB):
            b_tail_step(3, tb)

        P.dma("sp", cc_in.ap(), SUMM, "ccin")
        cci = cc_in.ap()
        cco = cc_out.ap()
        if NOCC:
            P.dma("sp", cco[0:128, :], cci, "cc")
        else:
            P.wait_for("pool", [k[2] for kk, k in P.dma_keys.items() if k[2] is not None and k[2].eng == "pool"])
            ccop = P.dma("pool", cco, cci, "cc", inc=1,
                         fn=lambda e: e.collective_compute("AllGather", ALU.bypass, replica_groups=[list(range(g0, g0 + CCG)) for g0 in range(0, NCORES, CCG)],
                                                           ins=[cc_in.ap().opt()], outs=[cc_out.ap().opt()]))
        P.dma("sp", GATH, cco.rearrange("(r p) c -> p r c", p=128), "gath")
        def emit_fold():
            M8 = CM[:, 2:2 + 4 * CCG].rearrange("p (k c) -> p k c", k=CCG)
            NM8 = CM[:, 34:34 + 4 * CCG].rearrange("p (k c) -> p k c", k=CCG)
            GA_ = GATH[:, :, 0:4]
            GH_ = GATH[:, :, 4:8]
            P.op("dve", lambda e: e.tensor_tensor(out=APR, in0=GA_, in1=M8, op=ALU.mult), [GA_, M8], [APR])
            P.op("dve", lambda e: e.tensor_tensor(out=APR, in0=APR, in1=NM8, op=ALU.add), [APR, NM8], [APR])
            P.op("dve", lambda e: e.tensor_tensor(out=HPR, in0=GH_, in1=M8, op=ALU.mult), [GH_, M8], [HPR])
            for c in range(4):
                d0 = APR[:, :, c]
                d1 = HPR[:, :, c]
                fo = FOLD[:, c, :]
                P.op("dve", lambda e, d0=d0, d1=d1, fo=fo: e.tensor_tensor_scan(out=fo, data0=d0, data1=d1, initial=0.0, op0=ALU.mult, op1=ALU.add),
                     [d0, d1], [fo])

        HIN = lambda c: FOLD[:, c, CCG - 1:CCG]

        A.cur = SCR
        YC = A.alloc([8, 512], BF16)
        UPB = [A.alloc([HALO + 512], F32) for _ in range(2)]
        SB_ = [A.alloc([HALO + 512], F32) for _ in range(2)]
        SBP = [A.alloc([HALO + 512], F32) for _ in range(2)]
        DBFS = [A.alloc([512], BF16) for _ in range(2)]
        T16 = A.alloc([16], F32)
        HS = A.alloc([512], F32)
        GGS = [A.alloc([512], F32) for _ in range(2)]
        SQC = SBP[0].bitcast(BF16)[:, 0:D]
        WO = SLOT[2].rearrange("p (k n) -> p k n", k=KC)
        if WMODE == "sw":
            P.dma("pool", WO, w_out_d.rearrange("(k p) n -> p k n", p=128), "w_s2")
        else:
            enqueue(WO, w_out_d, KC)
        load_wup(0)
        load_wdn(0)
        pump(8)
        Y = YC
        L = HALO + 512

        def c_s1(item):
            kind, tb, c, bi = item
            g = c
            pump(1) if kind == "lru" else None
            if kind == "lru":
                GG = GGS[bi % 2]
                pg = bankA()
                for kc in range(KC):
                    l_ = WIG[:, kc, c * 128:(c + 1) * 128]
                    r_ = ZT[:, kc, tb * 512:(tb + 1) * 512]
                    P.op("pe", lambda e, l_=l_, r_=r_, pg=pg, kc=kc: e.matmul(pg, l_, r_, start=(kc == 0), stop=(kc == KC - 1)), [l_, r_], [pg])
                P.op("act", lambda e, pg=pg, GG=GG: e.activation(out=GG, in_=pg, func=AF.Gelu_apprx_tanh), [pg], [GG])
                return
            UP = UPB[bi % 2]
            if tb == 0:
                pz = bankA()
                for kc in range(KC):
                    l_ = WIP[:, kc, g * 128:(g + 1) * 128]
                    r_ = ZTH[:, kc, :]
                    P.op("pe", lambda e, l_=l_, r_=r_, pz=pz, kc=kc: e.matmul(pz[:, 0:HALO], l_, r_, start=(kc == 0), stop=(kc == KC - 1)),
                         [l_, r_], [pz[:, 0:HALO]])
                P.op("act", lambda e, pz=pz, UP=UP: e.copy(out=UP[:, 0:HALO], in_=pz[:, 0:HALO]), [pz[:, 0:HALO]], [UP[:, 0:HALO]])
            pz = bankA()
            for kc in range(KC):
                l_ = WIP[:, kc, g * 128:(g + 1) * 128]
                r_ = ZT[:, kc, tb * 512:(tb + 1) * 512]
                P.op("pe", lambda e, l_=l_, r_=r_, pz=pz, kc=kc: e.matmul(pz, l_, r_, start=(kc == 0), stop=(kc == KC - 1)), [l_, r_], [pz])
            ub = UP[:, HALO:HALO + 512]
            P.op("act", lambda e, pz=pz, ub=ub: e.copy(out=ub, in_=pz), [pz], [ub])

        def c_s2(item):
            kind, tb, c, bi = item
            g = c
            if kind == "lru":
                GG = GGS[bi % 2]
                ab, bb = Ablk(c, tb), Bblk(c, tb)
                init = HIN(c) if tb == 0 else col(CAR, c)
                P.op("dve", lambda e, ab=ab, bb=bb, init=init: e.tensor_tensor_scan(out=HS, data0=ab, data1=bb, initial=init, op0=ALU.mult, op1=ALU.add),
                     [ab, bb, init], [HS])
                if tb < NB - 1:
                    cc_ = col(CAR, c)
                    P.op("dve", lambda e, cc_=cc_: e.tensor_copy(out=cc_, in_=HS[:, 511:512]), [HS[:, 511:512]], [cc_])
                yl = Y[:, 4 + c, :]
                P.op("pool", lambda e, yl=yl, GG=GG: e.tensor_tensor(out=yl, in0=HS, in1=GG, op=ALU.mult), [HS, GG], [yl])
                return
            w = 2 ** (g + 1)
            UP = UPB[bi % 2]
            DBF = DBFS[bi % 2]
            ub = UP[:, HALO:HALO + 512]
            if tb > 0:
                tl = TAIL[:, g, :]
                P.op("dve", lambda e, UP=UP, tl=tl: e.tensor_copy(out=UP[:, 0:HALO], in_=tl), [tl], [UP[:, 0:HALO]])
            if tb < NB - 1:
                tl = TAIL[:, g, :]
                ue = UP[:, 512:512 + HALO]
                P.op("dve", lambda e, tl=tl, ue=ue: e.tensor_copy(out=tl, in_=ue), [ue], [tl])
            eng = "pool" if g >= 2 else "dve"
            bufs = SBP if g >= 2 else SB_
            src = UP
            lo = 0
            for k in range(g + 1):
                sh = 2 ** k
                dst = bufs[k % 2]
                lo2 = lo + sh
                o_ = dst[:, lo2:L]
                i0 = src[:, lo2:L]
                i1 = src[:, lo2 - sh:L - sh]
                P.op(eng, lambda e, o_=o_, i0=i0, i1=i1: e.tensor_tensor(out=o_, in0=i0, in1=i1, op=ALU.add), [i0, i1], [o_])
                src = dst
                lo = lo2
            s_ = src[:, HALO:L]
            P.op("dve", lambda e, s_=s_, ub=ub, w=w, DBF=DBF: e.scalar_tensor_tensor(out=DBF, in0=s_, scalar=1.0 / w, in1=ub, op0=ALU.mult, op1=ALU.subtract),
                 [s_, ub], [DBF])
            if tb == 0:
                s16 = src[:, HALO:2 * HALO]
                iv = INVC[:, g, :]
                u16 = UP[:, HALO:2 * HALO]
                P.op("dve", lambda e, s16=s16, iv=iv: e.tensor_tensor(out=T16, in0=s16, in1=iv, op=ALU.mult), [s16, iv], [T16])
                P.op("dve", lambda e, u16=u16, DBF=DBF: e.tensor_tensor(out=DBF[:, 0:HALO], in0=T16, in1=u16, op=ALU.subtract), [T16, u16], [DBF[:, 0:HALO]])
            pp_ = bankA()
            pw = PW[:, g, :]
            P.op("pe", lambda e, pp_=pp_, pw=pw, DBF=DBF: e.matmul(pp_, pw, DBF, start=True, stop=True), [pw, DBF], [pp_])
            yp = Y[:, g, :]
            P.op("dve", lambda e, yp=yp, pp_=pp_, g=g: e.tensor_scalar(out=yp, in0=pp_, scalar1=PB_(g), scalar2=PSC(g), op0=ALU.add, op1=ALU.mult),
                 [pp_, VEC], [yp])

        def c_outproj(tb):
            for j in range(4):
                tt = tb * 4 + j
                s = xr_next()
                P.dma("sp", XR[s], x_d[tt * 128:(tt + 1) * 128, :], f"xr{s}")
                for nh in range(2):
                    po = bankB()
                    for c8 in range(8):
                        l_ = Y[:, c8, j * 128:(j + 1) * 128]
                        r_ = WO[:, c8, nh * 512:(nh + 1) * 512]
                        P.op("pe", lambda e, l_=l_, r_=r_, po=po, c8=c8: e.matmul(po, l_, r_, start=(c8 == 0), stop=(c8 == 7)), [l_, r_], [po])
                    hd = H[:, tt, nh * 512:(nh + 1) * 512]
                    xs = XR[s][:, nh * 512:(nh + 1) * 512]
                    P.op("dve", lambda e, hd=hd, po=po, xs=xs: e.tensor_tensor(out=hd, in0=po, in1=xs, op=ALU.add), [po, xs], [hd])
                sumsq(H[:, tt, :], col(SS, tt), SQC)

        items = []
        npool = nlru = 0
        for tb in range(NB):
            for c in range(4):
                items.append(("pool", tb, c, npool))
                npool += 1
            for c in range(4):
                items.append(("lru", tb, c, nlru))
                nlru += 1
        c_s1(items[0])
        for i, it in enumerate(items):
            if i + 1 < len(items):
                c_s1(items[i + 1])
            if i == 4:
                emit_fold()
            c_s2(it)
            if i % 8 == 7:
                c_outproj(i // 8)
        load_wup(1)
        load_wdn(1)
    else:
        for tt in range(NT):
            P.dma("sp", H[:, tt, :], x_d[tt * 128:(tt + 1) * 128, :], f"xr{tt % 2}")
        load_wup(0)
        load_wdn(0)
        load_wup(1)
        load_wdn(1)
        pump(32)

    def dump_h():
        for tt in range(NT):
            P.dma("sp", out_d[tt * 128:(tt + 1) * 128, :], H[:, tt, :], f"od{tt % 2}", is_output=True)

    if stage == "CW":
        for rep in range(6):
            load_wup(1)
            load_wdn(1)
        stage = "C"
    if stage == "C":
        dump_h()
        P.emit()
        return nc, P, A

    A.cur = SCR
    ZB = [A.alloc([D], BF16) for _ in range(2)]
    SQ = A.alloc([D], BF16)
    RT = [A.alloc([512], F32) for _ in range(3)]
    HID = [A.alloc([8, 512], BF16) for _ in range(2)]
    P.dma("sp", GS, gbc_d[:, D:2 * D], "gs")

    def norm_all_to_zt(gain):
        for tt in range(NT):
            sumsq(H[:, tt, :], col(SS, tt), SQ)
        rstd_from(SS[:, 0:NT], RMS[:, 0:NT], RSTD[:, 0:NT])

    if MONLY:
        norm_all_to_zt(GS)
    else:
        rstd_from(SS[:, 0:NT], RMS[:, 0:NT], RSTD[:, 0:NT])
    for tt in range(NT):
        s = tt % 2
        pump(1)
        scale_to(ZB[s], H[:, tt, :], col(RSTD, tt), GS)
        transposes_to(ZB[s], ZT[:, :, tt * 128:(tt + 1) * 128])

    rti = [0]

    def mlp_up(g, tb):
        hid = HID[(g * NB + tb) % 2]
        for fc in range(8):
            pu = bankA()
            for kc in range(KC):
                l_ = WUP[g][:, kc, fc * 128:(fc + 1) * 128]
                r_ = ZT[:, kc, tb * 512:(tb + 1) * 512]
                P.op("pe", lambda e, l_=l_, r_=r_, pu=pu, kc=kc: e.matmul(pu, l_, r_, start=(kc == 0), stop=(kc == KC - 1)), [l_, r_], [pu])
            rt = RT[rti[0] % 3]
            rti[0] += 1
            P.op("act", lambda e, rt=rt, pu=pu: e.activation(out=rt, in_=pu, func=AF.Relu), [pu], [rt])
            hd = hid[:, fc, :]
            if SQ_ENG == "pool":
                P.op("pool", lambda e, hd=hd, rt=rt: e.tensor_tensor(out=hd, in0=rt, in1=rt, op=ALU.mult), [rt], [hd])
            elif SQ_ENG == "act":
                P.op("act", lambda e, hd=hd, rt=rt: e.activation(out=hd, in_=rt, func=AF.Square), [rt], [hd])
            else:
                P.op("dve", lambda e, hd=hd, rt=rt: e.tensor_tensor(out=hd, in0=rt, in1=rt, op=ALU.mult), [rt], [hd])

    def mlp_down(g, tb):
        hid = HID[(g * NB + tb) % 2]
        for j in range(4):
            tt = tb * 4 + j
            for nh in range(2):
                pd = bankB()
                for fc in range(8):
                    l_ = hid[:, fc, j * 128:(j + 1) * 128]
                    r_ = WDN[g][:, fc, nh * 512:(nh + 1) * 512]
                    P.op("pe", lambda e, l_=l_, r_=r_, pd=pd, fc=fc: e.matmul(pd, l_, r_, start=(fc == 0), stop=(fc == 7)), [l_, r_], [pd])
                hd = H[:, tt, nh * 512:(nh + 1) * 512]
                P.op("dve", lambda e, hd=hd, pd=pd: e.tensor_tensor(out=hd, in0=pd, in1=hd, op=ALU.add), [pd, hd], [hd])
            if g == 3 and stage == "full":
                sumsq(H[:, tt, :], col(SS, tt), SQ)

    seq = [(g, tb) for g in range(4) for tb in range(NB)]
    if stage.startswith("Dg"):
        gl = [int(ch) for ch in stage[2:]]
        seq = [(g, tb) for g in gl for tb in range(NB)]
        for g in gl:
            if g >= 2:
                load_wup(g)
                load_wdn(g)
    elif stage[0] in "DM" and len(stage) > 1:
        seq = seq[:int(stage[1:])]
    if seq:
        mlp_up(*seq[0])
    for i, (g, tb) in enumerate(seq):
        if stage.startswith("Dg"):
            if i + 1 < len(seq):
                mlp_up(*seq[i + 1])
            mlp_down(g, tb)
            continue
        pump(4)
        if i + 1 < len(seq):
            mlp_up(*seq[i + 1])
        mlp_down(g, tb)
        if tb == NB - 1:
            if g + 2 < 4:
                load_wup(g + 2)
                load_wdn(g + 2)
            elif g == 2:
                WPG = SLOT[up_slot[2]].rearrange("p (k n) -> p k n", k=KC)
                WPP = SLOT[dn_slot[2]][:, 0:2048].rearrange("p (k n) -> p k n", k=2)
                if WMODE == "sw":
                    P.dma("pool", WPG, w_pg_d.rearrange("(k p) n -> p k n", p=128), f"w_s{up_slot[2]}")
                    P.dma("pool", WPP, w_pp_d.rearrange("(k p) n -> p k n", p=128), f"w_s{dn_slot[2]}")
                else:
                    enqueue(WPG, w_pg_d, KC)
                    enqueue(WPP, w_pp_d, 2)

    pump(len(WQ))
    if stage[0] in "DM":
        dump_h()
        P.emit()
        return nc, P, A

    A.cur = SCR
    o_zb = A.cur
    ZB = [A.alloc([D], BF16) for _ in range(2)]
    SQ = A.alloc([D], BF16)
    GF = A.alloc([D], F32)
    o_bpl = A.cur
    BPL = A.alloc([D], F32)
    o_z3 = A.cur
    Z3T = [A.alloc([KC, 128], BF16) for _ in range(3)]
    PT = [A.alloc([PLE], F32) for _ in range(2)]
    PBF = [A.alloc([PLE], BF16) for _ in range(2)]
    PTT = [A.alloc([2, 128], BF16) for _ in range(3)]
    GT = A.alloc([D], F32)
    GTS = [GT, XR[0]]
    P.dma("sp", GS, gbc_d[:, 2 * D:3 * D], "gs")
    P.dma("sp", GF, gbc_d[:, 3 * D:4 * D], "gf")
    P.dma("sp", BPL, gbc_d[:, 4 * D:5 * D], "bpl")
    rstd_from(SS[:, 0:NT], RMS[:, 0:NT], RSTD[:, 0:NT])

    def e_stage1a(tt):
        s = tt % 2
        scale_to(ZB[s], H[:, tt, :], col(RSTD, tt), GS)
        P.dma("sp", PT[s], p_d[tt * 128:(tt + 1) * 128, :], f"pt{s}")
        P.op("pool", lambda e, s=s: e.tensor_copy(out=PBF[s], in_=PT[s]), [PT[s]], [PBF[s]])

    def e_stage1b(tt):
        s = tt % 2
        s3 = tt % 3
        transposes_to(ZB[s], Z3T[s3])
        transposes_to(PBF[s], PTT[s3], nchunk=2)

    def e_stage2(tt):
        s = tt % 3
        GTt = GTS[tt % 2]
        pgs, pqs = [], []
        for nh in range(2):
            pg = bankB()
            for kc in range(KC):
                l_ = Z3T[s][:, kc, :]
                r_ = WPG[:, kc, nh * 512:(nh + 1) * 512]
                P.op("pe", lambda e, l_=l_, r_=r_, pg=pg, kc=kc: e.matmul(pg, l_, r_, start=(kc == 0), stop=(kc == KC - 1)), [l_, r_], [pg])
            pgs.append(pg)
        for nh in range(2):
            gt = GTt[:, nh * 512:(nh + 1) * 512]
            bp = BPL[:, nh * 512:(nh + 1) * 512]
            pg = pgs[nh]
            P.op("dve", lambda e, gt=gt, pg=pg, bp=bp: e.tensor_tensor(out=gt, in0=pg, in1=bp, op=ALU.add), [pg, bp], [gt])
        pq = bankA()
        for nh in range(2):
            pqn = pq if nh == 0 else bankA()
            for k2 in range(2):
                l_ = PTT[s][:, k2, :]
                r_ = WPP[:, k2, nh * 512:(nh + 1) * 512]
                P.op("pe", lambda e, l_=l_, r_=r_, pqn=pqn, k2=k2: e.matmul(pqn, l_, r_, start=(k2 == 0), stop=(k2 == 1)), [l_, r_], [pqn])
            pqs.append(pqn)
        for nh in range(2):
            gt = GTt[:, nh * 512:(nh + 1) * 512]
            P.op("act", lambda e, gt=gt: e.activation(out=gt, in_=gt, func=AF.Sigmoid), [gt], [gt])
        for nh in range(2):
            gt = GTt[:, nh * 512:(nh + 1) * 512]
            pqn = pqs[nh]
            P.op("dve", lambda e, pqn=pqn, gt=gt: e.tensor_tensor(out=gt, in0=pqn, in1=gt, op=ALU.mult), [pqn, gt], [gt])
        for nh in range(2):
            gt = GTt[:, nh * 512:(nh + 1) * 512]
            hd = H[:, tt, nh * 512:(nh + 1) * 512]
            P.op("pool", lambda e, hd=hd, gt=gt: e.tensor_tensor(out=hd, in0=hd, in1=gt, op=ALU.add), [hd, gt], [hd])
        if tt > 0:
            sumsq(H[:, tt - 1, :], col(SS, 16 + tt - 1), SQ)

    e_stage1a(0)
    e_stage1b(0)
    e_stage1a(1)
    e_stage1b(1)
    e_stage1a(2)
    for tt in range(NT):
        if tt + 3 < NT:
            e_stage1a(tt + 3)
        e_stage2(tt)
        if tt + 2 < NT:
            e_stage1b(tt + 2)
    sumsq(H[:, NT - 1, :], col(SS, 16 + NT - 1), SQ)
    rstd_from(SS[:, 16:16 + NT], RMS[:, 16:16 + NT], RSTD[:, 16:16 + NT])
    OUTS = [XR[0], XR[1], GT, A.alloc([D], F32, at=o_zb), A.alloc([D], F32, at=o_bpl), A.alloc([D], F32, at=o_z3)]
    for tt in range(NT):
        so = OUTS[tt % len(OUTS)]
        scale_to(so, H[:, tt, :], col(RSTD, 16 + tt), GF)
        P.dma("sp", out_d[tt * 128:(tt + 1) * 128, :], so, f"ox{tt % len(OUTS)}", is_output=True)
    P.emit()
    return nc, P, A


def make_in_maps(x, p, norm_mix_g, w_in, pool_w, pool_b, pool_scale, conv_w, conv_b,
                 gate_a_w, gate_a_b, gate_x_w, gate_x_b, lru_L, w_out, norm_mlp_g,
                 w_up, w_down, norm_ple_g, w_ple_gate, b_ple_gate, w_ple_proj, norm_final_g):
    f = lambda a: np.ascontiguousarray(np.asarray(a, dtype=np.float32))
    x = f(x)
    p = f(p)
    B, S, _ = x.shape
    nchunk = NCORES // B

    def bd(wg):
        wg = f(wg)[0]
        o = np.zeros((4, 128, 128), np.float32)
        for c in range(4):
            o[c, 0:64, 0:64] = wg[2 * c]
            o[c, 64:128, 64:128] = wg[2 * c + 1]
        return o

    def pc(v, n):
        return f(v).reshape(n, 128).T

    vecs = np.zeros((128, 40), np.float32)
    vecs[:, 0:4] = pc(f(pool_b)[0].reshape(-1), 4)
    vecs[:, 4:8] = pc(f(pool_scale)[0], 4)
    cw = f(conv_w)[0]
    for c in range(4):
        for k in range(4):
            vecs[:, 8 + c * 4 + k] = cw[k, c * 128:(c + 1) * 128]
    vecs[:, 24:28] = pc(f(conv_b)[0], 4)
    vecs[:, 28:32] = pc(f(gate_a_b)[0].reshape(-1), 4)
    vecs[:, 32:36] = pc(f(gate_x_b)[0].reshape(-1), 4)
    vecs[:, 36:40] = pc(f(lru_L)[0], 4)
    gbc = np.concatenate([f(norm_mix_g)[0], f(norm_mlp_g)[0], f(norm_ple_g)[0], f(norm_final_g), f(b_ple_gate)[0]])
    gbc = np.ascontiguousarray(np.broadcast_to(gbc[None, :], (128, 5 * D)))
    shared = {
        "w_in": f(w_in)[0], "w_out": f(w_out)[0], "w_up": f(w_up)[0], "w_down": f(w_down)[0],
        "w_pg": f(w_ple_gate)[0], "w_pp": f(w_ple_proj)[0], "pool_w": f(pool_w)[0],
        "gaw": bd(gate_a_w), "gxw": bd(gate_x_w), "vecs": vecs, "gbc": gbc,
        "ident": np.eye(128, dtype=np.float32),
    }
    maps = []
    for r in range(NCORES):
        b, j = r // nchunk, r % nchunk
        lo = j * T
        xs = x[b, lo:lo + T]
        xh = x[b, lo - HALO:lo] if j > 0 else np.zeros((HALO, D), np.float32)
        cm = np.zeros((128, 66), np.float32)
        first = 1.0 if j == 0 else 0.0
        cm[:, 0] = first
        cm[:, 1] = 1.0 - first
        for k in range(CCG):
            kg = (r // CCG) * CCG + k
            m = 1.0 if (kg // nchunk == b and kg % nchunk < j) else 0.0
            cm[:, 2 + k * 4:2 + k * 4 + 4] = m
            cm[:, 34 + k * 4:34 + k * 4 + 4] = 1.0 - m
        ic = np.zeros((128, 64), np.float32)
        for g in range(4):
            w = 2 ** (g + 1)
            for t in range(16):
                ic[:, g * 16 + t] = 1.0 / min(lo + t + 1, w)
        m = dict(shared)
        m.update({"x": np.ascontiguousarray(xs), "xh": np.ascontiguousarray(xh),
                  "p": np.ascontiguousarray(p[0, b, lo:lo + T]), "cmask": cm, "invc": ic})
        maps.append(m)
    return maps, B, nchunk


_NC_CACHE = {}


def kernel(**inputs):
    maps, B, nchunk = make_in_maps(**inputs)
    if "full" not in _NC_CACHE:
        _NC_CACHE["full"] = build_nc("full")[0]
    nc = _NC_CACHE["full"]
    res = run_bass_kernel_spmd(nc, maps, core_ids=list(range(NCORES)))
    out = np.empty((B, nchunk * T, D), np.float32)
    for r in range(NCORES):
        b, j = r // nchunk, r % nchunk
        out[b, j * T:(j + 1) * T] = res.results[r]["out"]
    return out
```
